# Optimizing a Trainium2 kernel written in Bass

```python
import math
import jax, jax.numpy as jnp
from jax import lax
import numpy as np

D_MODEL = 1024
BATCH = 8
SEQ = 4096
DEPTH = 2

HEAD_DIM = 64
MLA_HEADS = D_MODEL // 2 // HEAD_DIM
MLA_NOPE = 64
MLA_ROPE = 32
MLA_QK = MLA_NOPE + MLA_ROPE
MLA_V = HEAD_DIM
Q_LORA = 3 * D_MODEL // 8
KV_LORA = D_MODEL // 4
ROPE_THETA = 10000.0
Q_BLOCK = 128

DIL_HEADS = D_MODEL // 2 // HEAD_DIM
DIL_HD = HEAD_DIM
DIL_PATTERNS = ((128, 1), (512, 4), (2048, 16))
DIL_Q_BLOCK = 64
N_BUCKETS = 32
MAX_DIST = 1024

MIX_WIDTH = MLA_HEADS * MLA_V + DIL_HEADS * DIL_HD
IN_SPLITS = (Q_LORA, KV_LORA, MLA_ROPE, DIL_HEADS * DIL_HD, DIL_HEADS * DIL_HD, DIL_HEADS * DIL_HD)
IN_WIDTH = sum(IN_SPLITS)

D_FF = 2816
CONV_W = 3
EPS = 1e-6

kernel_name = "hybrid_mla_dilated_convffn_encoder"


def rmsnorm(x, g):
    xf = x.astype(jnp.float32)
    y = xf * lax.rsqrt(jnp.mean(xf * xf, axis=-1, keepdims=True) + EPS)
    return (y * g.astype(jnp.float32)).astype(x.dtype)


def rope_tables(positions):
    half = MLA_ROPE // 2
    inv_freq = ROPE_THETA ** (-jnp.arange(half, dtype=jnp.float32) / half)
    ang = positions.astype(jnp.float32)[..., None] * inv_freq
    return jnp.cos(ang)[:, :, None, :], jnp.sin(ang)[:, :, None, :]


def apply_rope(x, cos, sin):
    xf = x.astype(jnp.float32)
    x1, x2 = jnp.split(xf, 2, axis=-1)
    return jnp.concatenate([x1 * cos - x2 * sin, x2 * cos + x1 * sin], axis=-1).astype(x.dtype)


def t5_bucket(rel):
    nb = N_BUCKETS // 2
    max_exact = nb // 2
    ret = (rel > 0).astype(jnp.int32) * nb
    n = jnp.abs(rel)
    large = max_exact + (jnp.log(jnp.maximum(n, 1).astype(jnp.float32) / max_exact)
                         / math.log(MAX_DIST / max_exact) * (nb - max_exact)).astype(jnp.int32)
    large = jnp.minimum(large, nb - 1)
    return ret + jnp.where(n < max_exact, n, large)


def dense_attention(q, k, v):
    B, S, H, Dq = q.shape
    nb = S // Q_BLOCK
    qb = q.reshape(B, nb, Q_BLOCK, H, Dq).transpose(1, 0, 2, 3, 4)

    def one(qblk):
        s = jnp.einsum('bqhd,bkhd->bhqk', qblk, k, preferred_element_type=jnp.float32)
        p = jax.nn.softmax(s, axis=-1).astype(v.dtype)
        return jnp.einsum('bhqk,bkhd->bqhd', p, v)

    o = lax.map(one, qb)
    return o.transpose(1, 0, 2, 3, 4).reshape(B, S, H, v.shape[-1])


def mla_mixer(c_q, c_kv, k_rope_in, cos, sin, g_cq, g_ckv, w_uq, w_ukv, g_qn, g_kn):
    B, S, _ = c_q.shape
    q = (rmsnorm(c_q, g_cq) @ w_uq).reshape(B, S, MLA_HEADS, MLA_QK)
    kv = (rmsnorm(c_kv, g_ckv) @ w_ukv).reshape(B, S, MLA_HEADS, MLA_NOPE + MLA_V)
    k_nope, v = kv[..., :MLA_NOPE], kv[..., MLA_NOPE:]
    k_rope = jnp.broadcast_to(k_rope_in[:, :, None, :], (B, S, MLA_HEADS, MLA_ROPE))
    k = jnp.concatenate([k_nope, k_rope], axis=-1)
    q = rmsnorm(q, g_qn)
    k = rmsnorm(k, g_kn)
    q = jnp.concatenate([q[..., :MLA_NOPE], apply_rope(q[..., MLA_NOPE:], cos, sin)], axis=-1)
    k = jnp.concatenate([k[..., :MLA_NOPE], apply_rope(k[..., MLA_NOPE:], cos, sin)], axis=-1)
    return dense_attention(q * (MLA_QK ** -0.5), k, v)


def dilated_attention(q, k, v, rel_bias):
    B, S, H, Dh = q.shape
    nb = S // DIL_Q_BLOCK
    patterns = []
    for (w, d) in DIL_PATTERNS:
        half = w // (2 * d)
        rel = d * jnp.arange(-half, half + 1, dtype=jnp.int32)
        patterns.append((rel, rel_bias[t5_bucket(rel)].T))
    qb = (q * (Dh ** -0.5)).reshape(B, nb, DIL_Q_BLOCK, H, Dh).transpose(1, 0, 2, 3, 4)
    starts = jnp.arange(nb, dtype=jnp.int32) * DIL_Q_BLOCK

    def one(args):
        qblk, s0 = args
        qpos = s0 + jnp.arange(DIL_Q_BLOCK, dtype=jnp.int32)
        outs, lses = [], []
        for rel, bias in patterns:
            kpos = qpos[:, None] + rel[None, :]
            valid = (kpos >= 0) & (kpos < S)
            idx = jnp.clip(kpos, 0, S - 1)
            kg = jnp.take(k, idx, axis=1)
            vg = jnp.take(v, idx, axis=1)
            s = jnp.einsum('bqhd,bqnhd->bhqn', qblk, kg, preferred_element_type=jnp.float32)
            s = s + bias.astype(jnp.float32)[None, :, None, :]
            s = jnp.where(valid[None, None], s, -jnp.inf)
            m = jnp.max(s, axis=-1, keepdims=True)
            e = jnp.exp(s - m)
            den = jnp.sum(e, axis=-1, keepdims=True)
            o = jnp.einsum('bhqn,bqnhd->bqhd', (e / den).astype(v.dtype), vg)
            outs.append(o.astype(jnp.float32))
            lses.append((m + jnp.log(den))[..., 0])
        wts = jax.nn.softmax(jnp.stack(lses, 0), axis=0)
        wts = wts.transpose(0, 1, 3, 2)[..., None]
        return jnp.sum(wts * jnp.stack(outs, 0), axis=0).astype(q.dtype)

    o = lax.map(one, (qb, starts))
    return o.transpose(1, 0, 2, 3, 4).reshape(B, S, H, Dh)


def conv_ffn(h, w_up, conv_w, conv_b, w_down):
    u = h @ w_up
    u = lax.conv_general_dilated(u, conv_w[:, None, :].astype(u.dtype), window_strides=(1,),
                                 padding=((CONV_W // 2, CONV_W // 2),),
                                 dimension_numbers=('NWC', 'WIO', 'NWC'),
                                 feature_group_count=2 * D_FF) + conv_b
    g, up = jnp.split(u, 2, axis=-1)
    return (jax.nn.silu(g) * up) @ w_down


def setup_inputs(seed: int = 0) -> dict:
    key = jax.random.key(seed)
    ks = jax.random.split(key, 24)
    f32 = jnp.float32
    nrm = lambda k, shape, s: jax.random.normal(k, shape, f32) * s
    gain = lambda k, shape: 1.0 + 0.02 * jax.random.normal(k, shape, f32)
    L = DEPTH
    x = jax.random.normal(ks[0], (BATCH, SEQ, D_MODEL), f32)
    positions = jnp.arange(SEQ, dtype=jnp.int32)[None, :] + jax.random.randint(ks[1], (BATCH, 1), 0, 1024, dtype=jnp.int32)
    return {
        "x": x,
        "positions": positions,
        "rel_bias": nrm(ks[2], (N_BUCKETS, DIL_HEADS), 0.5),
        "attn_norm": gain(ks[3], (L, D_MODEL)),
        "w_in": nrm(ks[4], (L, D_MODEL, IN_WIDTH), D_MODEL ** -0.5),
        "g_cq": gain(ks[5], (L, Q_LORA)),
        "g_ckv": gain(ks[6], (L, KV_LORA)),
        "w_uq": nrm(ks[7], (L, Q_LORA, MLA_HEADS * MLA_QK), Q_LORA ** -0.5),
        "w_ukv": nrm(ks[8], (L, KV_LORA, MLA_HEADS * (MLA_NOPE + MLA_V)), KV_LORA ** -0.5),
        "g_mla_q": gain(ks[9], (L, MLA_QK)),
        "g_mla_k": gain(ks[10], (L, MLA_QK)),
        "g_dil_q": gain(ks[11], (L, DIL_HD)),
        "g_dil_k": gain(ks[12], (L, DIL_HD)),
        "w_out": nrm(ks[13], (L, MIX_WIDTH, D_MODEL), MIX_WIDTH ** -0.5),
        "ffn_norm": gain(ks[14], (L, D_MODEL)),
        "w_up": nrm(ks[15], (L, D_MODEL, 2 * D_FF), D_MODEL ** -0.5),
        "conv_w": nrm(ks[16], (L, CONV_W, 2 * D_FF), CONV_W ** -0.5),
        "conv_b": nrm(ks[17], (L, 2 * D_FF), 0.01),
        "w_down": nrm(ks[18], (L, D_FF, D_MODEL), D_FF ** -0.5),
    }


def reference(x, positions, rel_bias, attn_norm, w_in, g_cq, g_ckv, w_uq, w_ukv, g_mla_q, g_mla_k,
              g_dil_q, g_dil_k, w_out, ffn_norm, w_up, conv_w, conv_b, w_down):
    B, S, _ = x.shape
    cos, sin = rope_tables(positions)
    split_at = list(np.cumsum(IN_SPLITS)[:-1])
    for l in range(DEPTH):
        h = rmsnorm(x, attn_norm[l])
        c_q, c_kv, k_r, q_d, k_d, v_d = jnp.split(h @ w_in[l], split_at, axis=-1)
        o_a = mla_mixer(c_q, c_kv, k_r, cos, sin, g_cq[l], g_ckv[l], w_uq[l], w_ukv[l],
                        g_mla_q[l], g_mla_k[l])
        q_d = rmsnorm(q_d.reshape(B, S, DIL_HEADS, DIL_HD), g_dil_q[l])
        k_d = rmsnorm(k_d.reshape(B, S, DIL_HEADS, DIL_HD), g_dil_k[l])
        v_d = v_d.reshape(B, S, DIL_HEADS, DIL_HD)
        o_b = dilated_attention(q_d, k_d, v_d, rel_bias)
        mixed = jnp.concatenate([o_a.reshape(B, S, -1), o_b.reshape(B, S, -1)], axis=-1)
        x = x + mixed @ w_out[l]
        x = x + conv_ffn(rmsnorm(x, ffn_norm[l]), w_up[l], conv_w[l], conv_b[l], w_down[l])
    return x
```

```python
import math
import numpy as np
import concourse.bass as bass
import concourse.mybir as mybir
from concourse.bass_utils import run_bass_kernel_spmd

F32 = mybir.dt.float32
BF16 = mybir.dt.bfloat16
I32 = mybir.dt.int32
ALU = mybir.AluOpType
AF = mybir.ActivationFunctionType
AX = mybir.AxisListType

S_LEN = 4096
D = 1024
NT = 32
EPS = 1e-6
TWO_PI = float(2 * np.pi)
NBIG = 105984


class T:
    def __init__(self, name, ap=None):
        self.name = name
        self.ap = ap
        self.writers = []
        self.readers = []
        self.prev = []
        self.dsem = None
        self.dcount = 0
        self.excl = False

    def __getitem__(self, k):
        return self.ap[k]


class Sched:
    ENG = ("pe", "act", "dve", "pool", "sp")

    def __init__(self, nc):
        self.nc = nc
        self.sem = {k: nc.alloc_semaphore("s_" + k) for k in ("pe", "act", "dve", "pool")}
        self.cnt = {k: 0 for k in self.sem}
        self.streams = {k: [] for k in self.ENG}
        self.known = {k: {} for k in self.ENG}
        self.semobj = {}
        self.bar = []
        self.dma_ts = []
        self.free_dsems = {"hw": [], "sw": []}
        self.nsem_alloc = 0
        self.all_dsems = {}
        self.nops = 0

    def _dsem(self, t, kind):
        if t.dsem is None:
            fl = self.free_dsems[kind]
            if fl:
                t.dsem, t.dcount = fl.pop()
            else:
                self.nsem_alloc += 1
                t.dsem = self.nc.alloc_semaphore("d%s%d" % (kind, self.nsem_alloc))
                t.dcount = 0
            t.dkind = kind
            self.dma_ts.append(t)
        assert t.dkind == kind, "tile %s gets DMAs from both HW and SW queues" % t.name
        return t.dsem

    def op(self, eng, fn, reads=(), writes=(), dma_dst=None, pwrites=()):
        deps = list(self.bar)
        excl = []
        for t in list(reads) + list(writes) + list(pwrites):
            if t.excl and t not in excl:
                excl.append(t)
        if excl:
            reads = [t for t in reads if not t.excl]
            writes = [t for t in writes if not t.excl]
            pwrites = [t for t in pwrites if not t.excl]
            for t in excl:
                deps += t.writers
        for t in reads:
            deps += t.writers
        for t in writes:
            if t.readers:
                t.prev = t.readers + t.writers
                t.readers = []
                t.writers = []
            deps += t.prev + t.writers
        for t in pwrites:
            if t.readers:
                t.prev = t.readers + t.writers
                t.readers = []
                t.writers = []
            deps += t.prev
        writes = list(writes) + list(pwrites)
        if dma_dst is not None:
            s = self._dsem(dma_dst, "sw" if eng == "pool" else "hw")
            dma_dst.dcount += 16
            tok = (s, dma_dst.dcount, 16)
            self.all_dsems[id(s)] = [s, dma_dst.dcount]
        else:
            self.cnt[eng] += 1
            tok = (self.sem[eng], self.cnt[eng], 1)
        for t in reads:
            t.readers.append(tok)
        for t in writes:
            t.writers.append(tok)
        for t in excl:
            t.writers = [tok]
        need = {}
        for (s, v, _) in deps:
            if eng == "pe" and s is self.sem["pe"]:
                continue
            k = id(s)
            self.semobj[k] = s
            if v > need.get(k, 0):
                need[k] = v
        waits = []
        kn = self.known[eng]
        for k, v in need.items():
            if kn.get(k, 0) >= v:
                continue
            kn[k] = v
            waits.append((self.semobj[k], v))
        self.streams[eng].append((waits, fn, tok))
        self.nops += 1
        return tok

    def barrier(self):
        toks = [(self.sem[k], self.cnt[k], 1) for k in self.sem if self.cnt[k] > 0]
        for k, (s, v) in self.all_dsems.items():
            toks.append((s, v, 16))
        self.bar = toks
        for t in self.dma_ts:
            self.free_dsems[t.dkind].append((t.dsem, t.dcount))
            t.dsem = None
        self.dma_ts = []

    def emit(self):
        nc = self.nc
        sched = self
        final = [(s, v) for (s, v) in self.all_dsems.values()]

        def run(e, key):
            for waits, fn, tok in sched.streams[key]:
                for (s, v) in waits:
                    e.wait_ge(s, v)
                ins = fn(e)
                ins.then_inc(tok[0], tok[2])
            if key == "sp":
                for (s, v) in final:
                    e.wait_ge(s, v)

        with nc.Block() as block:
            @block.tensor
            def _(e):
                run(e, "pe")

            @block.scalar
            def _(e):
                run(e, "act")

            @block.vector
            def _(e):
                run(e, "dve")

            @block.gpsimd
            def _(e):
                run(e, "pool")

            @block.sync
            def _(e):
                run(e, "sp")


class Mem:
    def __init__(self, nc):
        self.big = nc.alloc_sbuf_tensor("big", [128, NBIG], BF16).ap()
        self.off = 0
        self.n = 0

    def mark(self):
        return self.off

    def release(self, m):
        self.off = m

    def alloc(self, name, shape, dt):
        esz = 2 if dt == BF16 else 4
        nb = int(np.prod(shape[1:])) * esz
        off = (self.off + 31) // 32 * 32
        assert off + nb <= NBIG * 2, "SBUF overflow at %s: %d" % (name, off + nb)
        self.off = off + nb
        v = self.big[0:shape[0], off // 2:(off + nb) // 2]
        if dt != BF16:
            v = v.bitcast(dt)
        if len(shape) == 3:
            v = v.rearrange("p (a b) -> p a b", b=shape[2])
        elif len(shape) == 4:
            v = v.rearrange("p (a b c) -> p a b c", b=shape[2], c=shape[3])
        self.n += 1
        return T("%s_%d" % (name, self.n), v)

    def ring(self, name, n, shape, dt):
        return [self.alloc(name, shape, dt) for _ in range(n)]


def h3(ap, f):
    return ap.rearrange("p (h f) -> p h f", f=f)


class K:
    def __init__(self, nlayers=2, stop_after=None, dbg=False):
        self.nlayers = nlayers
        self.stop_after = stop_after
        self.dbg = dbg
        nc = self.nc = bass.Bass("TRN2", target_bir_lowering=False)
        self.S = Sched(nc)
        self.M = Mem(nc)
        L = 2

        def din(name, shape, dt=F32):
            return nc.dram_tensor(name, shape, dt, kind="ExternalInput").ap()

        self.x = din("x", [S_LEN, D])
        self.pos = din("pos", [128, 32], I32)
        self.biasM = din("biasM", [3, 128, 8 * 384])
        self.invf = din("invf", [16])
        self.attn_norm = din("attn_norm", [L, D])
        self.w_in = din("w_in", [L, D, 2208])
        self.g_cq = din("g_cq", [L, 384])
        self.g_ckv = din("g_ckv", [L, 256])
        self.w_uq = din("w_uq", [L, 384, 768])
        self.w_ukv = din("w_ukv", [L, 256, 1024])
        self.g_mla_q = din("g_mla_q", [L, 96])
        self.g_mla_k = din("g_mla_k", [L, 96])
        self.g_dil_q = din("g_dil_q", [L, 64])
        self.g_dil_k = din("g_dil_k", [L, 64])
        self.w_out = din("w_out", [L, D, D])
        self.ffn_norm = din("ffn_norm", [L, D])
        self.w_up = din("w_up", [L, D, 5632])
        self.conv_w = din("conv_w", [L, 128, 3 * 44])
        self.conv_b = din("conv_b", [L, 128, 44])
        self.w_down = din("w_down", [L, 2816, D])
        self.out = nc.dram_tensor("out", [S_LEN, D], F32, kind="ExternalOutput").ap()
        kind = "ExternalOutput" if dbg else "Internal"

        def scr(name, shape):
            return nc.dram_tensor(name, shape, BF16, kind=kind).ap()

        self.QT = scr("QT", [8, 96, S_LEN])
        self.KT = scr("KT", [8, 96, S_LEN])
        self.VA = scr("VA", [S_LEN, 520])
        self.QdT = scr("QdT", [4, 128, S_LEN])
        self.KdT = scr("KdT", [4, 128, S_LEN])
        self.VdA = scr("VdA", [S_LEN, 520])
        self.MT = scr("MT", [8, 128, S_LEN])
        self.RD = nc.dram_tensor("RD", [64, 512], F32, kind="Internal").ap()
        self.RD2 = nc.dram_tensor("RD2", [2 * S_LEN], F32, kind="Internal").ap()
        self.tRD = [T("RDa"), T("RDb")]
        self.tRD2 = T("RD2")
        self.RD3 = nc.dram_tensor("RD3", [2 * S_LEN], F32, kind="Internal").ap()
        self.tRD3 = T("RD3")
        self.tQT, self.tKT, self.tVA = T("QT"), T("KT"), T("VA")
        self.tQdT, self.tKdT, self.tVdA, self.tMT = T("QdT"), T("KdT"), T("VdA"), T("MT")
        self.tXin = T("Xin")
        self.outv = 0
        self.tOut = T("out0")
        pall = nc.alloc_psum_tensor("pall", [128, 4096], F32).ap()
        self.pall = pall
        self.bank = [T("bank%d" % i, pall[:, 512 * i:512 * (i + 1)]) for i in range(8)]
        for b_ in self.bank:
            b_.excl = True

    def new_out(self):
        self.outv += 1
        self.tOut = T("out%d" % self.outv)
        return self.tOut

    def I(self, eng, name, reads, writes, **kw):
        return self.S.op(eng, lambda e: getattr(e, name)(**kw), reads, writes)

    def act(self, out, in_, func, reads, writes, **kw):
        return self.S.op("act", lambda e: e.activation(out=out, in_=in_, func=func, **kw), reads, writes)

    def dma(self, q, out, in_, reads, writes, dst, **kw):
        return self.S.op(q, lambda e: e.dma_start(out=out, in_=in_, **kw), reads, (), dma_dst=dst, pwrites=writes)

    def mm(self, groups, reads, writes):
        def fn(e):
            ins = None
            for out_ap, pairs in groups:
                n = len(pairs)
                for i, (l, r) in enumerate(pairs):
                    ins = e.matmul(out_ap, lhsT=l, rhs=r, start=(i == 0), stop=(i == n - 1))
            return ins
        return self.S.op("pe", fn, reads, writes)

    def tr(self, items, reads, writes):
        ident = self.ident

        def fn(e):
            ins = None
            for o, i in items:
                ins = e.transpose(out=o, in_=i, identity=ident.ap[:])
            return ins
        return self.S.op("pe", fn, list(reads) + [ident], writes)

    def bank_bf(self, i):
        return self.bank[i].ap.bitcast(BF16)

    def setup(self):
        S, M, nc = self.S, self.M, self.nc
        self.ident = M.alloc("ident", [128, 128], BF16)
        self.onesf = M.alloc("onesf", [128, 64], F32)
        self.cosT = M.alloc("cosT", [128, 512], F32)
        self.sinT = M.alloc("sinT", [128, 512], F32)
        m = M.mark()
        identf = M.alloc("identf", [128, 128], F32)
        self.I("pool", "memset", [], [identf], ap=identf[:], constant=0.0)
        self.I("pool", "affine_select", [identf], [identf], out=identf[:], in_=identf[:], pattern=[[-1, 128]],
               compare_op=ALU.not_equal, fill=1.0, base=0, channel_multiplier=1)
        self.I("dve", "tensor_copy", [identf], [self.ident], out=self.ident[:], in_=identf[:])
        self.I("pool", "memset", [], [self.onesf], ap=self.onesf[:], constant=1.0)
        pi_ = M.alloc("pi", [128, 32], I32)
        pf = M.alloc("pf", [128, 32], F32)
        ivf = M.alloc("ivf", [128, 16], F32)
        ang = M.alloc("ang", [128, 512], F32)
        tq = M.alloc("tq", [128, 512], F32)
        ki = M.alloc("ki", [128, 512], I32)
        kf = M.alloc("kf", [128, 512], F32)
        r = M.alloc("r", [128, 512], F32)
        self.dma("sp", pi_[:], self.pos[:, :], [], [pi_], pi_)
        self.dma("sp", ivf[:], self.invf.partition_broadcast(128), [], [ivf], ivf)
        self.I("dve", "tensor_copy", [pi_], [pf], out=pf[:], in_=pi_[:])
        self.I("dve", "tensor_tensor", [pf, ivf], [ang], out=h3(ang[:], 16),
               in0=pf[:].unsqueeze(2).to_broadcast([128, 32, 16]),
               in1=ivf[:].unsqueeze(1).to_broadcast([128, 32, 16]), op=ALU.mult)

        def trig(dst, shift):
            self.I("dve", "tensor_scalar", [ang], [tq], out=tq[:], in0=ang[:], scalar1=shift, scalar2=1.0 / TWO_PI,
                   op0=ALU.add, op1=ALU.mult)
            self.I("dve", "tensor_copy", [tq], [ki], out=ki[:], in_=tq[:])
            self.I("dve", "tensor_copy", [ki], [kf], out=kf[:], in_=ki[:])
            self.I("dve", "scalar_tensor_tensor", [kf, ang], [r], out=r[:], in0=kf[:], scalar=-TWO_PI, in1=ang[:],
                   op0=ALU.mult, op1=ALU.add)
            self.I("dve", "tensor_scalar", [r], [r], out=r[:], in0=r[:], scalar1=shift, scalar2=None, op0=ALU.add)
            self.I("dve", "tensor_scalar", [r], [tq], out=tq[:], in0=r[:], scalar1=float(np.pi), scalar2=-TWO_PI,
                   op0=ALU.is_gt, op1=ALU.mult)
            self.I("dve", "tensor_tensor", [r, tq], [r], out=r[:], in0=r[:], in1=tq[:], op=ALU.add)
            self.act(dst[:], r[:], AF.Sin, [r], [dst], scale=0.999999)

        trig(self.sinT, 0.0)
        trig(self.cosT, float(np.pi / 2))
        S.barrier()
        M.release(m)

    def phase_a(self, l):
        S, M, nc = self.S, self.M, self.nc
        bank = self.bank
        m0 = M.mark()
        xsrc, txsrc = (self.x, self.tXin) if l == 0 else (self.out, self.tOut)
        w_in_sb = M.alloc("w_in", [128, 8, 2208], BF16)
        w_uq_sb = M.alloc("w_uq", [128, 3, 768], BF16)
        w_ukv_sb = M.alloc("w_ukv", [128, 2, 1024], BF16)
        for k in range(8):
            self.dma("pool", w_in_sb[:, k, :], self.w_in[l, k * 128:(k + 1) * 128, :], [], [w_in_sb], w_in_sb)
        self.dma("pool", w_uq_sb[:], self.w_uq[l].rearrange("(k p) n -> p k n", p=128), [], [w_uq_sb], w_uq_sb)
        self.dma("pool", w_ukv_sb[:], self.w_ukv[l].rearrange("(k p) n -> p k n", p=128), [], [w_ukv_sb], w_ukv_sb)

        def gain(name, src, n):
            t = M.alloc(name, [128, n], F32)
            self.dma("sp", t[:], src.partition_broadcast(128), [], [t], t)
            return t
        gA = gain("gA", self.attn_norm[l], D)
        gcl = M.alloc("gcl", [128, 640], F32)
        self.dma("sp", gcl[:, 0:384], self.g_cq[l].partition_broadcast(128), [], [gcl], gcl)
        self.dma("sp", gcl[:, 384:640], self.g_ckv[l].partition_broadcast(128), [], [gcl], gcl)
        gq = gain("gq", self.g_mla_q[l], 96)
        gk = gain("gk", self.g_mla_k[l], 96)
        gdq = gain("gdq", self.g_dil_q[l], 64)
        gdk = gain("gdk", self.g_dil_k[l], 64)
        self.I("dve", "tensor_scalar", [gdk], [gdk], out=gdk[:], in0=gdk[:], scalar1=8.0, scalar2=None, op0=ALU.mult)
        self.I("dve", "tensor_tensor", [gdk, gdq], [gdk], out=gdk[:], in0=gdk[:], in1=gdq[:], op=ALU.mult)
        self.I("dve", "tensor_scalar", [gk], [gk], out=gk[:], in0=gk[:], scalar1=float(math.sqrt(96.0)), scalar2=None,
               op0=ALU.mult)
        def rope_tabs(g, nm):
            AC = M.alloc("AC" + nm, [128, 32, 32], F32)
            BD = M.alloc("BD" + nm, [128, 32, 32], F32)
            c3 = h3(self.cosT[:], 16)
            s3 = h3(self.sinT[:], 16)
            g1 = g[:, 64:80].unsqueeze(1).to_broadcast([128, 32, 16])
            g2 = g[:, 80:96].unsqueeze(1).to_broadcast([128, 32, 16])
            rd = [g, self.cosT, self.sinT]
            self.I("dve", "tensor_tensor", rd, [AC], out=AC[:, :, 0:16], in0=c3, in1=g1, op=ALU.mult)
            self.I("dve", "tensor_tensor", rd, [AC], out=AC[:, :, 16:32], in0=c3, in1=g2, op=ALU.mult)
            self.I("dve", "scalar_tensor_tensor", rd, [BD], out=BD[:, :, 0:16], in0=s3, scalar=-1.0, in1=g2,
                   op0=ALU.mult, op1=ALU.mult)
            self.I("dve", "tensor_tensor", rd, [BD], out=BD[:, :, 16:32], in0=s3, in1=g1, op=ALU.mult)
            return AC, BD
        ACq, BDq = rope_tabs(gq, "q")
        ACk, BDk = rope_tabs(gk, "k")

        xt = M.ring("xt", 4, [128, D], F32)
        junk = M.alloc("junk", [128, D], F32)
        st = M.ring("st", 2, [128, 32], F32)
        hb = M.ring("hb", 3, [128, D], BF16)
        hT = M.ring("hT", 2, [128, 8, 128], BF16)
        cln = M.ring("cln", 2, [128, 640], BF16)
        cT = M.ring("cT", 2, [128, 5, 128], BF16)
        kr = M.ring("kr", 2, [128, 32], F32)
        kru = M.ring("kru", 2, [128, 32], F32)
        krv = M.ring("krv", 2, [128, 32], F32)
        krr = M.ring("krr", 2, [128, 32], F32)
        qsq = M.alloc("qsq", [128, 768], F32)
        qn = M.alloc("qn", [128, 768], F32)
        ru = M.alloc("ru", [128, 8, 32], F32)
        rv = M.alloc("rv", [128, 8, 32], F32)
        ksq = M.alloc("ksq", [128, 512], F32)
        kn = M.alloc("kn", [128, 512], F32)
        dsq = M.ring("dsq", 2, [128, 512], F32)
        dn = M.ring("dn", 2, [128, 512], F32)
        qf = M.ring("qf", 2, [128, 768], BF16)
        kfb = M.ring("kfb", 2, [128, 768], BF16)
        qdf = M.ring("qdf", 3, [128, 512], BF16)
        kdf = M.ring("kdf", 3, [128, 512], BF16)
        QTg = M.ring("QTg", 2, [128, 8, 512], BF16)
        KTg = M.ring("KTg", 2, [128, 8, 512], BF16)
        QdTg = M.ring("QdTg", 2, [128, 4, 512], BF16)
        KdTg = M.ring("KdTg", 2, [128, 4, 512], BF16)
        VAs = M.ring("VAs", 2, [128, 8, 65], BF16)
        VdAs = M.ring("VdAs", 2, [128, 8, 65], BF16)
        for t_ in VAs + VdAs:
            self.I("pool", "memset", [], [t_], ap=t_[:], constant=1.0)
        st2 = M.ring("st2", 2, [128, 32], F32)
        TPB = 0
        tpb = self.bank_bf(TPB)
        pq = self.pall[:, 3072:3840]
        pkv = self.pall[:, 3072:4096]
        bq = [bank[6], bank[7]]

        sf = M.ring("sf", 4, [128, 2], F32)

        def front_elem(t):
            x_t, s_t, hb_t = xt[t % 4], sf[t % 4], hb[t % 3]
            if t == 0:
                for tt_ in range(3):
                    self.dma("sp", xt[tt_ % 4][:], xsrc[tt_ * 128:(tt_ + 1) * 128, :], [txsrc], [xt[tt_ % 4]], xt[tt_ % 4])
            if t + 3 < NT:
                tt_ = t + 3
                self.dma("sp", xt[tt_ % 4][:], xsrc[tt_ * 128:(tt_ + 1) * 128, :], [txsrc], [xt[tt_ % 4]], xt[tt_ % 4])
            self.act(junk[:], x_t[:], AF.Square, [x_t], [s_t, junk], accum_out=s_t[:, 0:1])
            self.act(s_t[:, 0:1], s_t[:, 0:1], AF.Sqrt, [s_t], [s_t], bias=EPS, scale=1.0 / D)
            self.I("dve", "reciprocal", [s_t], [s_t], out=s_t[:, 1:2], in_=s_t[:, 0:1])
            self.I("dve", "scalar_tensor_tensor", [x_t, s_t, gA], [hb_t], out=hb_t[:], in0=x_t[:], scalar=s_t[:, 1:2],
                   in1=gA[:], op0=ALU.mult, op1=ALU.mult)

        def front_pe(t):
            hb_t, hT_t = hb[t % 3], hT[t % 2]
            self.tr([(tpb[:, k * 128:(k + 1) * 128], hb_t[:, k * 128:(k + 1) * 128]) for k in range(8)], [hb_t], [bank[TPB]])
            self.act(hT_t[:].rearrange("p a b -> p (a b)"), tpb[:, 0:1024], AF.Copy, [bank[TPB]], [hT_t])

        def wstage(t):
            s_t, s2_t, hT_t, cln_t = st[t % 2], st2[t % 2], hT[t % 2], cln[t % 2]
            cols = [(1, 0, 384), (2, 384, 672), (3, 672, 1184), (4, 1184, 1696), (5, 1696, 2208)]
            for (bi, c0, c1) in cols:
                self.mm([(bank[bi][:, 0:c1 - c0], [(hT_t[:, k, :], w_in_sb[:, k, c0:c1]) for k in range(8)])],
                        [hT_t, w_in_sb], [bank[bi]])
            self.act(junk[:, 0:384], bank[1][:, 0:384], AF.Square, [bank[1]], [s_t, junk], accum_out=s_t[:, 2:3])
            self.act(junk[:, 0:256], bank[2][:, 0:256], AF.Square, [bank[2]], [s_t, junk], accum_out=s_t[:, 3:4])
            self.act(junk[:, 0:32], bank[2][:, 256:288], AF.Square, [bank[2]], [s_t, junk], accum_out=s_t[:, 6:7])
            self.act(s_t[:, 2:3], s_t[:, 2:3], AF.Sqrt, [s_t], [s_t], bias=EPS, scale=1.0 / 384)
            self.act(s_t[:, 3:4], s_t[:, 3:4], AF.Sqrt, [s_t], [s_t], bias=EPS, scale=1.0 / 256)
            self.I("dve", "reciprocal", [s_t], [s_t], out=s_t[:, 4:6], in_=s_t[:, 2:4])
            self.I("dve", "tensor_scalar", [s_t], [s_t], out=s_t[:, 7:8], in0=s_t[:, 6:7], scalar1=96 * EPS, scalar2=None,
                   op0=ALU.add)
            self.I("dve", "scalar_tensor_tensor", [bank[1], s_t, gcl], [cln_t], out=cln_t[:, 0:384], in0=bank[1][:, 0:384],
                   scalar=s_t[:, 4:5], in1=gcl[:, 0:384], op0=ALU.mult, op1=ALU.mult)
            self.I("dve", "scalar_tensor_tensor", [bank[2], s_t, gcl], [cln_t], out=cln_t[:, 384:640],
                   in0=bank[2][:, 0:256], scalar=s_t[:, 5:6], in1=gcl[:, 384:640], op0=ALU.mult, op1=ALU.mult)
            kr_t, kru_t, krv_t, krr_t = kr[t % 2], kru[t % 2], krv[t % 2], krr[t % 2]
            self.I("dve", "tensor_copy", [bank[2]], [kr_t], out=kr_t[:], in_=bank[2][:, 256:288])
            self.I("pool", "tensor_tensor", [kr_t, ACk], [kru_t], out=kru_t[:], in0=kr_t[:], in1=ACk[:, t, :], op=ALU.mult)
            self.I("pool", "tensor_tensor", [kr_t, BDk], [krv_t], out=krv_t[:, 0:16], in0=kr_t[:, 16:32], in1=BDk[:, t, 0:16],
                   op=ALU.mult)
            self.I("pool", "tensor_tensor", [kr_t, BDk], [krv_t], out=krv_t[:, 16:32], in0=kr_t[:, 0:16],
                   in1=BDk[:, t, 16:32], op=ALU.mult)
            self.I("pool", "tensor_tensor", [kru_t, krv_t], [krr_t], out=krr_t[:], in0=kru_t[:], in1=krv_t[:], op=ALU.add)
            for (bi, which) in ((3, 0), (4, 1)):
                dsq_t, dn_t = dsq[which], dn[which]
                gt = gdq if which == 0 else gdk
                dst = qdf[t % 3] if which == 0 else kdf[t % 3]
                so = 8 * which
                self.act(dsq_t[:], bank[bi][:, 0:512], AF.Square, [bank[bi]], [dsq_t])
                self.I("dve", "tensor_reduce", [dsq_t], [s2_t], out=s2_t[:, so:so + 8], in_=h3(dsq_t[:], 64), axis=AX.X,
                       op=ALU.add)
                self.act(s2_t[:, so:so + 8], s2_t[:, so:so + 8], AF.Sqrt, [s2_t], [s2_t], bias=64 * EPS, scale=1.0)
                self.I("dve", "reciprocal", [s2_t], [s2_t], out=s2_t[:, 16 + so:24 + so], in_=s2_t[:, so:so + 8])
                if which == 0:
                    self.I("dve", "tensor_tensor", [bank[bi], s2_t], [dst], out=h3(dst[:], 64), in0=h3(bank[bi][:, 0:512], 64),
                           in1=s2_t[:, 16 + so:24 + so].unsqueeze(2).to_broadcast([128, 8, 64]), op=ALU.mult)
                else:
                    self.I("dve", "tensor_tensor", [bank[bi], s2_t], [dn_t], out=h3(dn_t[:], 64), in0=h3(bank[bi][:, 0:512], 64),
                           in1=s2_t[:, 16 + so:24 + so].unsqueeze(2).to_broadcast([128, 8, 64]), op=ALU.mult)
                    self.I("pool", "tensor_tensor", [dn_t, gt], [dst], out=h3(dst[:], 64), in0=h3(dn_t[:], 64),
                           in1=gt[:].unsqueeze(1).to_broadcast([128, 8, 64]), op=ALU.mult)
            VdA_t = VdAs[t % 2]
            self.act(VdA_t[:, :, 0:64], h3(bank[5][:, 0:512], 64), AF.Copy, [bank[5]], [VdA_t])
            self.dma("sp", self.VdA[t * 128:(t + 1) * 128, :], VdA_t[:].rearrange("p a b -> p (a b)"), [VdA_t], [self.tVdA],
                     VdA_t)

        def t2stage(t):
            cln_t, cT_t = cln[t % 2], cT[t % 2]
            self.tr([(tpb[:, k * 128:(k + 1) * 128], cln_t[:, k * 128:(k + 1) * 128]) for k in range(5)], [cln_t], [bank[TPB]])
            self.I("dve", "tensor_copy", [bank[TPB]], [cT_t], out=cT_t[:].rearrange("p a b -> p (a b)"), in_=tpb[:, 0:640])

        def qstage(t):
            s_t, cT_t, qf_t = st[t % 2], cT[t % 2], qf[t % 2]
            self.mm([(self.pall[:, 3072:3584], [(cT_t[:, k, :], w_uq_sb[:, k, 0:512]) for k in range(3)]),
                     (self.pall[:, 3584:3840], [(cT_t[:, k, :], w_uq_sb[:, k, 512:768]) for k in range(3)])],
                    [cT_t, w_uq_sb], bq)
            self.act(qsq[:], pq, AF.Square, bq, [qsq])
            self.I("dve", "tensor_reduce", [qsq], [s_t], out=s_t[:, 8:16], in_=h3(qsq[:], 96), axis=AX.X, op=ALU.add)
            self.act(s_t[:, 8:16], s_t[:, 8:16], AF.Sqrt, [s_t], [s_t], bias=96 * EPS, scale=1.0)
            self.I("dve", "reciprocal", [s_t], [s_t], out=s_t[:, 8:16], in_=s_t[:, 8:16])
            self.I("dve", "tensor_tensor", bq + [s_t], [qn], out=h3(qn[:], 96), in0=h3(pq, 96),
                   in1=s_t[:, 8:16].unsqueeze(2).to_broadcast([128, 8, 96]), op=ALU.mult)
            qn3 = h3(qn[:], 96)
            qf3 = h3(qf_t[:], 96)
            self.I("pool", "tensor_tensor", [qn, gq], [qf_t], out=qf3[:, :, 0:64], in0=qn3[:, :, 0:64],
                   in1=gq[:, 0:64].unsqueeze(1).to_broadcast([128, 8, 64]), op=ALU.mult)
            self.I("pool", "tensor_tensor", [qn, ACq], [ru], out=ru[:], in0=qn3[:, :, 64:96],
                   in1=ACq[:, t, :].unsqueeze(1).to_broadcast([128, 8, 32]), op=ALU.mult)
            self.I("pool", "tensor_tensor", [qn, BDq], [rv], out=rv[:, :, 0:16], in0=qn3[:, :, 80:96],
                   in1=BDq[:, t, 0:16].unsqueeze(1).to_broadcast([128, 8, 16]), op=ALU.mult)
            self.I("pool", "tensor_tensor", [qn, BDq], [rv], out=rv[:, :, 16:32], in0=qn3[:, :, 64:80],
                   in1=BDq[:, t, 16:32].unsqueeze(1).to_broadcast([128, 8, 16]), op=ALU.mult)
            self.I("dve", "tensor_tensor", [ru, rv], [qf_t], out=qf3[:, :, 64:96], in0=ru[:], in1=rv[:], op=ALU.add)

        def kvstage(t):
            s_t, cT_t, kf_t, krr_t = st[t % 2], cT[t % 2], kfb[t % 2], krr[t % 2]
            self.mm([(self.pall[:, 3072:3584], [(cT_t[:, 3 + k, :], w_ukv_sb[:, k, 0:512]) for k in range(2)]),
                     (self.pall[:, 3584:4096], [(cT_t[:, 3 + k, :], w_ukv_sb[:, k, 512:1024]) for k in range(2)])],
                    [cT_t, w_ukv_sb], bq)
            kv3 = h3(pkv, 128)
            self.act(h3(ksq[:], 64), kv3[:, :, 0:64], AF.Square, bq, [ksq])
            self.I("dve", "tensor_reduce", [ksq], [s_t], out=s_t[:, 16:24], in_=h3(ksq[:], 64), axis=AX.X, op=ALU.add)
            self.act(s_t[:, 16:24], s_t[:, 16:24], AF.Sqrt, [s_t], [s_t], bias=s_t[:, 7:8], scale=1.0)
            self.I("dve", "reciprocal", [s_t], [s_t], out=s_t[:, 16:24], in_=s_t[:, 16:24])
            self.I("dve", "tensor_tensor", bq + [s_t], [kn], out=h3(kn[:], 64), in0=kv3[:, :, 0:64],
                   in1=s_t[:, 16:24].unsqueeze(2).to_broadcast([128, 8, 64]), op=ALU.mult)
            kf3 = h3(kf_t[:], 96)
            self.I("pool", "tensor_tensor", [kn, gk], [kf_t], out=kf3[:, :, 0:64], in0=h3(kn[:], 64),
                   in1=gk[:, 0:64].unsqueeze(1).to_broadcast([128, 8, 64]), op=ALU.mult)
            self.I("pool", "tensor_tensor", [krr_t, s_t], [kf_t], out=kf3[:, :, 64:96],
                   in0=krr_t[:].unsqueeze(1).to_broadcast([128, 8, 32]),
                   in1=s_t[:, 16:24].unsqueeze(2).to_broadcast([128, 8, 32]), op=ALU.mult)
            VA_t = VAs[t % 2]
            self.act(VA_t[:, :, 0:64], kv3[:, :, 64:128], AF.Copy, bq, [VA_t])
            self.dma("sp", self.VA[t * 128:(t + 1) * 128, :], VA_t[:].rearrange("p a b -> p (a b)"), [VA_t], [self.tVA], VA_t)

        def t3q(t):
            g, gi = t // 4, t % 4
            gr, c0 = g % 2, (t % 4) * 128
            qf3 = h3(qf[t % 2][:], 96)
            self.tr([(tpb[0:96, h * 128:(h + 1) * 128], qf3[:, h, :]) for h in range(8)], [qf[t % 2]], [bank[TPB]])
            self.act(QTg[gr][0:96, :, c0:c0 + 128], h3(tpb[0:96, 0:1024], 128), AF.Copy, [bank[TPB]], [QTg[gr]])

        def t3k(t):
            g, gi = t // 4, t % 4
            gr, c0 = g % 2, (t % 4) * 128
            kf3 = h3(kfb[t % 2][:], 96)
            self.tr([(tpb[0:96, h * 128:(h + 1) * 128], kf3[:, h, :]) for h in range(8)], [kfb[t % 2]], [bank[TPB]])
            self.I("dve", "tensor_copy", [bank[TPB]], [KTg[gr]], out=KTg[gr][0:96, :, c0:c0 + 128], in_=h3(tpb[0:96, 0:1024], 128))

        def t3d(t):
            g, gi = t // 4, t % 4
            gr, c0 = g % 2, (t % 4) * 128
            qd_, kd_ = qdf[t % 3], kdf[t % 3]
            self.tr([(tpb[:, p * 128:(p + 1) * 128], qd_[:, p * 128:(p + 1) * 128]) for p in range(4)] +
                    [(tpb[:, 512 + p * 128:512 + (p + 1) * 128], kd_[:, p * 128:(p + 1) * 128]) for p in range(4)],
                    [qd_, kd_], [bank[TPB]])
            self.act(QdTg[gr][:, :, c0:c0 + 128], h3(tpb[:, 0:512], 128), AF.Copy, [bank[TPB]], [QdTg[gr]])
            self.I("dve", "tensor_copy", [bank[TPB]], [KdTg[gr]], out=KdTg[gr][:, :, c0:c0 + 128], in_=h3(tpb[:, 512:1024], 128))
            if gi == 3:
                tc = slice(g * 512, (g + 1) * 512)
                self.dma("sp", self.QT[:, :, tc].rearrange("h f t -> f h t"), QTg[gr][0:96], [QTg[gr]], [self.tQT], QTg[gr])
                self.dma("sp", self.KT[:, :, tc].rearrange("h f t -> f h t"), KTg[gr][0:96], [KTg[gr]], [self.tKT], KTg[gr])
                self.dma("sp", self.QdT[:, :, tc].rearrange("h f t -> f h t"), QdTg[gr][:], [QdTg[gr]], [self.tQdT], QdTg[gr])
                self.dma("sp", self.KdT[:, :, tc].rearrange("h f t -> f h t"), KdTg[gr][:], [KdTg[gr]], [self.tKdT], KdTg[gr])

        front_elem(0)
        front_elem(1)
        front_pe(0)
        wstage(0)
        for i in range(NT):
            if i + 1 < NT:
                front_pe(i + 1)
            if i >= 1:
                t3q(i - 1)
                t3k(i - 1)
                t3d(i - 1)
            t2stage(i)
            qstage(i)
            if i + 2 < NT:
                front_elem(i + 2)
            if i + 1 < NT:
                wstage(i + 1)
            kvstage(i)
        t3q(NT - 1)
        t3k(NT - 1)
        t3d(NT - 1)
        S.barrier()
        M.release(m0)

    def phase_mla(self, l):
        S, M, nc = self.S, self.M, self.nc
        bank = self.bank
        m0 = M.mark()
        VAsb = M.alloc("VAsb", [128, 32, 520], BF16)
        QTh = M.ring("QTh", 2, [96, S_LEN], BF16)
        KTh = M.ring("KTh", 2, [96, S_LEN], BF16)
        P = M.ring("P", 3, [128, 1024], BF16)
        rden = M.ring("rden", 2, [128, 512], F32)
        Osb = M.ring("Osb", 2, [64, 512], F32)
        mix = M.ring("mix", 2, [64, 512], BF16)
        steps = [(h, qb, k2) for h in range(8) for qb in range(8) for k2 in range(16)]
        nsteps = len(steps)
        SB = [(0, 1), (2, 3), (4, 5)]
        LA = 2
        rbc = M.ring("rbc", 2, [64, 512], F32)
        deferred = {}

        def load_head(h):
            self.dma("sp", QTh[h % 2][:], self.QT[h], [self.tQT], [QTh[h % 2]], QTh[h % 2])
            self.dma("sp", KTh[h % 2][:], self.KT[h], [self.tKT], [KTh[h % 2]], KTh[h % 2])

        def emit_s(i):
            h, qb, k2 = steps[i]
            sb2 = SB[i % 3]
            q_ = QTh[h % 2][:, qb * 512:(qb + 1) * 512]
            self.mm([(bank[sb2[j]][:, :], [(KTh[h % 2][:, (2 * k2 + j) * 128:(2 * k2 + j + 1) * 128], q_)]) for j in range(2)],
                    [KTh[h % 2], QTh[h % 2]], [bank[sb2[0]], bank[sb2[1]]])

        load_head(0)
        self.dma("sp", VAsb[:], self.VA.rearrange("(t p) f -> p t f", p=128), [self.tVA], [VAsb], VAsb)
        for i in range(LA):
            emit_s(i)
        for i in range(nsteps):
            h, qb, k2 = steps[i]
            blk = h * 8 + qb
            if qb == 0 and k2 == 0 and h + 1 < 8:
                load_head(h + 1)
            sb2 = SB[i % 3]
            p_ = P[i % 3]
            ob = bank[6 + blk % 2]
            self.act(p_[:], self.pall[:, sb2[0] * 512:sb2[0] * 512 + 1024], AF.Exp, [bank[sb2[0]], bank[sb2[1]]], [p_])
            if i + LA < nsteps:
                emit_s(i + LA)

            def pv(e, ob=ob, k2=k2, h=h, p_=p_):
                e.matmul(ob[0:65, :], lhsT=VAsb[:, 2 * k2, h * 65:(h + 1) * 65], rhs=p_[:, 0:512], start=(k2 == 0), stop=False)
                return e.matmul(ob[0:65, :], lhsT=VAsb[:, 2 * k2 + 1, h * 65:(h + 1) * 65], rhs=p_[:, 512:1024], start=False,
                                stop=(k2 == 15))
            self.S.op("pe", pv, [VAsb, p_], [ob])
            if k2 == 15:
                rd, os_, mx, rb = rden[blk % 2], Osb[blk % 2], mix[blk % 2], rbc[blk % 2]
                trd = self.tRD[blk % 2]
                self.I("dve", "reciprocal", [ob], [rd], out=rd[64:65, :], in_=ob[64:65, :])
                self.I("dve", "tensor_copy", [ob], [os_], out=os_[:], in_=ob[0:64, :])
                self.dma("sp", self.RD[blk:blk + 1, :], rd[64:65, :], [rd], [trd], rd)

                def epi(os_=os_, mx=mx, rb=rb, trd=trd, blk=blk, h=h, qb=qb):
                    self.dma("sp", rb[:], self.RD[blk].partition_broadcast(64), [trd], [rb], rb)
                    self.I("pool", "tensor_tensor", [os_, rb], [mx], out=mx[:], in0=os_[:], in1=rb[:], op=ALU.mult)
                    self.dma("sp", self.MT[h // 2, (h % 2) * 64:(h % 2) * 64 + 64, qb * 512:(qb + 1) * 512], mx[:], [mx],
                             [self.tMT], mx)
                deferred[i + 2] = epi
            if i in deferred:
                deferred.pop(i)()
        for k_ in sorted(deferred):
            deferred[k_]()
        S.barrier()
        M.release(m0)

    def phase_dil(self, l):
        S, M, nc = self.S, self.M, self.nc
        bank = self.bank
        m0 = M.mark()
        maskb = M.alloc("maskb", [128, 24, 384], BF16)
        rbc = M.alloc("rbcd", [64, 2 * S_LEN], F32)
        mixb = M.ring("mixb", 2, [64, S_LEN], BF16)
        mm_ = M.mark()
        mtmp = M.ring("mtmp", 2, [128, 8 * 384], F32)
        for d_ in range(3):
            mt = mtmp[d_ % 2]
            self.dma("sp", mt[:], self.biasM[d_], [], [mt], mt)
            self.act(maskb[:, d_ * 8:(d_ + 1) * 8, :].rearrange("p a b -> p (a b)"), mt[:], AF.Exp, [mt], [maskb])
        S.barrier()
        M.release(mm_)
        Qp = M.ring("Qp", 2, [128, S_LEN], BF16)
        Kp = M.ring("Kp", 2, [128, S_LEN], BF16)
        Vp = M.ring("Vp", 2, [128, 32, 130], BF16)
        Oaccs = M.ring("Oacc", 2, [65, 2, S_LEN], F32)
        dsq_ = M.alloc("dsq_", [128, 64], F32)
        Pd = M.ring("Pd", 3, [128, 2, 384], BF16)
        Pm = M.ring("Pm", 3, [128, 2, 384], BF16)
        DIL = (1, 4, 16)
        SB = [(0, 1), (2, 3), (4, 5)]
        LA = 2
        vload = 0
        for p in range(4):
            Oacc = Oaccs[p % 2]
            self.dma("sp", Qp[p % 2][:], self.QdT[p], [self.tQdT], [Qp[p % 2]], Qp[p % 2])
            self.dma("sp", Kp[p % 2][:], self.KdT[p], [self.tKdT], [Kp[p % 2]], Kp[p % 2])
            Q_, K_ = Qp[p % 2], Kp[p % 2]
            for di, d in enumerate(DIL):
                nb = 32 // d
                V_ = Vp[vload % 2]
                vload += 1
                vsrc = self.VdA.rearrange("(tt pp c) f -> c pp tt f", pp=128, c=d)
                for c in range(d):
                    self.dma("sp", V_[:, c * nb:(c + 1) * nb, :], vsrc[c][:, :, p * 130:(p + 1) * 130], [self.tVdA], [V_], V_)

                def tok_slice(blk, nb=nb, d=d):
                    c = blk // nb
                    j0 = (blk % nb) * 128
                    s0 = c + d * j0
                    return slice(s0, s0 + d * 127 + 1, d)

                def dl_range(qb, nb=nb):
                    j = qb % nb
                    return (0 if j > 0 else 1), (2 if j < nb - 1 else 1)

                def emit_s(qb, K_=K_, Q_=Q_, tok_slice=tok_slice, dl_range=dl_range):
                    sb2 = SB[qb % 3]
                    d0, d1 = dl_range(qb)
                    groups = []
                    for dl in range(d0, d1 + 1):
                        kb = qb + dl - 1
                        for hh in range(2):
                            pr = slice(64 * hh, 64 * hh + 64)
                            groups.append((bank[sb2[hh]][:, dl * 128:(dl + 1) * 128],
                                           [(K_[pr, tok_slice(kb)], Q_[pr, tok_slice(qb)])]))
                    self.mm(groups, [K_, Q_], [bank[sb2[0]], bank[sb2[1]]])

                for qb in range(LA):
                    emit_s(qb)
                pend = None
                for qb in range(32):
                    sb2 = SB[qb % 3]
                    pd, pm = Pd[qb % 3], Pm[qb % 3]
                    ob = bank[6 + qb % 2]
                    d0, d1 = dl_range(qb)
                    cs = slice(d0 * 128, (d1 + 1) * 128)
                    s2 = self.pall[:, sb2[0] * 512:sb2[0] * 512 + 1024].rearrange("p (a b) -> p a b", b=512)[:, :, cs]
                    self.act(pd[:, :, cs], s2, AF.Exp, [bank[sb2[0]], bank[sb2[1]]], [pd])
                    self.I("dve", "tensor_tensor", [pd, maskb], [pm], out=pm[:, :, cs], in0=pd[:, :, cs],
                           in1=maskb[:, di * 8 + 2 * p:di * 8 + 2 * p + 2, cs], op=ALU.mult)
                    if qb + LA < 32:
                        emit_s(qb + LA)
                    groups = []
                    for hh in range(2):
                        pairs = []
                        for dl in range(d0, d1 + 1):
                            kb = qb + dl - 1
                            pairs.append((V_[:, kb, hh * 65:(hh + 1) * 65], pm[:, hh, dl * 128:(dl + 1) * 128]))
                        groups.append((ob[0:65, hh * 128:(hh + 1) * 128], pairs))
                    self.mm(groups, [V_, pm], [ob])
                    if pend is not None:
                        pend()

                    def pend(ob=ob, ts=tok_slice(qb), di=di, Oacc=Oacc):
                        src = ob[0:65, 0:256].rearrange("p (a b) -> p a b", b=128)
                        if di == 0:
                            self.I("dve", "tensor_copy", [ob], [Oacc], out=Oacc[:, :, ts], in_=src)
                        else:
                            self.I("dve", "tensor_tensor", [ob, Oacc], [Oacc], out=Oacc[:, :, ts], in0=src, in1=Oacc[:, :, ts],
                                   op=ALU.add)
                pend()
            oa = Oacc
            self.dma("sp", self.RD2.rearrange("(o n) -> o n", o=1), oa[64:65, :, :].rearrange("p a b -> p (a b)"), [oa],
                     [self.tRD2], oa)
            self.dma("sp", dsq_[:], self.RD2.rearrange("(p n) -> p n", p=128), [self.tRD2], [dsq_], dsq_)
            self.I("dve", "reciprocal", [dsq_], [dsq_], out=dsq_[:], in_=dsq_[:])
            self.dma("sp", self.RD3.rearrange("(p n) -> p n", p=128), dsq_[:], [dsq_], [self.tRD3], dsq_)
            self.dma("sp", rbc[:], self.RD3.partition_broadcast(64), [self.tRD3], [rbc], rbc)
            for hh in range(2):
                mx = mixb[hh]
                eng = "pool" if hh == 0 else "dve"
                self.I(eng, "tensor_tensor", [oa, rbc], [mx], out=mx[:], in0=oa[0:64, hh, :],
                       in1=rbc[:, hh * S_LEN:(hh + 1) * S_LEN], op=ALU.mult)
                self.dma("sp", self.MT[4 + p, hh * 64:hh * 64 + 64, :], mx[:], [mx], [self.tMT], mx)
        S.barrier()
        M.release(m0)

    def phase_c(self, l):
        S, M, nc = self.S, self.M, self.nc
        bank = self.bank
        m0 = M.mark()
        xsrc, txsrc = (self.x, self.tXin) if l == 0 else (self.out, self.tOut)
        h2T = M.alloc("h2T", [128, 8, S_LEN], BF16)
        NQ = [(0, 6), (6, 12), (12, 17), (17, 22)]
        wu = M.ring("wu", 2, [128, 8, 2, 768], BF16)
        wd = M.ring("wd", 2, [128, 6, D], BF16)
        cw = M.alloc("cw", [128, 3, 44], F32)
        cb = M.alloc("cb", [128, 44], F32)
        self.dma("sp", cw[:].rearrange("p a b -> p (a b)"), self.conv_w[l], [], [cw], cw)
        self.dma("sp", cb[:], self.conv_b[l], [], [cb], cb)

        def load_q(qi):
            j0, j1 = NQ[qi]
            n = (j1 - j0) * 128
            w_, d_ = wu[qi % 2], wd[qi % 2]
            src = self.w_up[l].rearrange("(k p) n -> p k n", p=128)
            self.dma("pool", w_[:, :, 0, 0:n], src[:, :, j0 * 128:j1 * 128], [], [w_], w_)
            self.dma("pool", w_[:, :, 1, 0:n], src[:, :, 2816 + j0 * 128:2816 + j1 * 128], [], [w_], w_)
            self.dma("pool", d_[:, 0:j1 - j0, :], self.w_down[l, j0 * 128:j1 * 128, :].rearrange("(j p) n -> p j n", p=128),
                     [], [d_], d_)
        m1 = M.mark()
        w_out_sb = M.alloc("w_out", [128, 8, D], BF16)
        self.dma("pool", w_out_sb[:], self.w_out[l].rearrange("(k p) n -> p k n", p=128), [], [w_out_sb], w_out_sb)
        load_q(0)
        gF = M.alloc("gF", [128, D], F32)
        self.dma("sp", gF[:], self.ffn_norm[l].partition_broadcast(128), [], [gF], gF)
        MTg = M.ring("MTg", 2, [128, 8, 512], BF16)
        xt = M.ring("xtc", 3, [128, D], F32)
        x1 = M.ring("x1", 2, [128, D], F32)
        junk = M.alloc("junkc", [128, D], F32)
        st = M.ring("stc", 2, [128, 4], F32)
        hb = M.ring("hbc", 2, [128, D], BF16)
        tOutNew = self.new_out() if l > 0 else self.tOut
        told = txsrc
        def c1_front(t):
            g, gi = t // 4, t % 4
            mt = MTg[g % 2]
            if gi == 0:
                for g_ in ([0, 1] if g == 0 else [g + 1]):
                    if g_ < 8:
                        m_ = MTg[g_ % 2]
                        self.dma("sp", m_[:], self.MT[:, :, g_ * 512:(g_ + 1) * 512].rearrange("k p t -> p k t"), [self.tMT], [m_], m_)
            x_t, x1_t, s_t, hb_t = xt[t % 3], x1[t % 2], st[t % 2], hb[t % 2]
            if t == 0:
                for tt_ in range(2):
                    self.dma("sp", xt[tt_ % 3][:], xsrc[tt_ * 128:(tt_ + 1) * 128, :], [told], [xt[tt_ % 3]], xt[tt_ % 3])
            if t + 2 < NT:
                tt_ = t + 2
                self.dma("sp", xt[tt_ % 3][:], xsrc[tt_ * 128:(tt_ + 1) * 128, :], [told], [xt[tt_ % 3]], xt[tt_ % 3])
            yb = (1, 2) if t % 2 == 0 else (3, 4)
            yap = self.pall[:, yb[0] * 512:yb[0] * 512 + 1024]
            self.mm([(bank[yb[0]][:, :], [(mt[:, k, gi * 128:(gi + 1) * 128], w_out_sb[:, k, 0:512]) for k in range(8)]),
                     (bank[yb[1]][:, :], [(mt[:, k, gi * 128:(gi + 1) * 128], w_out_sb[:, k, 512:1024]) for k in range(8)])],
                    [mt, w_out_sb], [bank[yb[0]], bank[yb[1]]])
            self.I("dve", "tensor_tensor", [bank[yb[0]], bank[yb[1]], x_t], [x1_t], out=x1_t[:], in0=yap, in1=x_t[:], op=ALU.add)
            self.dma("sp", self.out[t * 128:(t + 1) * 128, :], x1_t[:], [x1_t], [tOutNew], x1_t)
            self.act(junk[:], x1_t[:], AF.Square, [x1_t], [s_t, junk], accum_out=s_t[:, 0:1])
            self.act(s_t[:, 0:1], s_t[:, 0:1], AF.Sqrt, [s_t], [s_t], bias=EPS, scale=1.0 / D)
            self.I("dve", "reciprocal", [s_t], [s_t], out=s_t[:, 1:2], in_=s_t[:, 0:1])
            self.I("dve", "scalar_tensor_tensor", [x1_t, s_t, gF], [hb_t], out=hb_t[:], in0=x1_t[:], scalar=s_t[:, 1:2],
                   in1=gF[:], op0=ALU.mult, op1=ALU.mult)

        def c1_back(t):
            hb_t = hb[t % 2]
            b = 0 if t % 2 == 0 else 7
            tpb = self.bank_bf(b)
            self.tr([(tpb[:, k * 128:(k + 1) * 128], hb_t[:, k * 128:(k + 1) * 128]) for k in range(8)], [hb_t], [bank[b]])
            self.act(h2T[:, :, t * 128:(t + 1) * 128], h3(tpb[:, 0:1024], 128), AF.Copy, [bank[b]], [h2T])

        c1_front(0)
        for t in range(NT):
            if t + 1 < NT:
                c1_front(t + 1)
            c1_back(t)
        self.tOut = tOutNew
        S.barrier()
        M.release(m1)
        if self.stop_after is not None and self.stop_after == (l, "c1"):
            M.release(m0)
            return
        aT = M.ring("aT", 2, [128, 6, 512], BF16)
        cg = M.ring("cg", 2, [128, 512], F32)
        cu = M.ring("cu", 2, [128, 512], F32)
        sg = M.ring("sg", 2, [128, 512], F32)
        xa = M.ring("xa", 3, [128, D], F32)
        xo = M.ring("xo", 2, [128, D], F32)
        GSZ = 510
        ngroups = (S_LEN + GSZ - 1) // GSZ
        self._it = 0
        ia = 0
        pend_tiles = []
        t0 = M.ring("t0", 2, [128, 512], F32)
        t2 = M.ring("t2", 2, [128, 512], F32)
        for qi, (j0, j1) in enumerate(NQ):
            if qi + 1 < len(NQ):
                load_q(qi + 1)
            w_, d_ = wu[qi % 2], wd[qi % 2]
            tin = self.tOut
            tnew = self.new_out()
            for g in range(ngroups):
                o0 = g * GSZ
                o1 = min(o0 + GSZ, S_LEN)
                c0 = max(o0 - 1, 0)
                c1 = min(o1 + 1, S_LEN)
                n = c1 - c0
                a_ = aT[ia % 2]
                ia += 1
                for j in range(j0, j1):
                    jj = j - j0
                    ug, uu = (bank[0], bank[1]) if j % 2 == 0 else (bank[2], bank[3])
                    self.mm([(ug[:, 0:n], [(w_[:, k, 0, jj * 128:(jj + 1) * 128], h2T[:, k, c0:c1]) for k in range(8)])],
                            [w_, h2T], [ug])
                    self.mm([(uu[:, 0:n], [(w_[:, k, 1, jj * 128:(jj + 1) * 128], h2T[:, k, c0:c1]) for k in range(8)])],
                            [w_, h2T], [uu])
                    cg_, cu_, sg_, t0_, t2_ = cg[j % 2], cu[j % 2], sg[j % 2], t0[j % 2], t2[j % 2]
                    ch = j
                    self.I("dve", "tensor_scalar", [ug, cw, cb], [cg_], out=cg_[:, 0:n], in0=ug[:, 0:n], scalar1=cw[:, 1, ch:ch + 1],
                           scalar2=cb[:, ch:ch + 1], op0=ALU.mult, op1=ALU.add)
                    self.I("dve", "scalar_tensor_tensor", [ug, cw, cg_], [cg_], out=cg_[:, 1:n], in0=ug[:, 0:n - 1],
                           scalar=cw[:, 0, ch:ch + 1], in1=cg_[:, 1:n], op0=ALU.mult, op1=ALU.add)
                    self.I("dve", "scalar_tensor_tensor", [ug, cw, cg_], [cg_], out=cg_[:, 0:n - 1], in0=ug[:, 1:n],
                           scalar=cw[:, 2, ch:ch + 1], in1=cg_[:, 0:n - 1], op0=ALU.mult, op1=ALU.add)
                    ch = 22 + j
                    self.act(cu_[:, 0:n], uu[:, 0:n], AF.Identity, [uu, cw, cb], [cu_], scale=cw[:, 1, ch:ch + 1],
                             bias=cb[:, ch:ch + 1])
                    self.act(t0_[:, 0:n], uu[:, 0:n], AF.Copy, [uu, cw], [t0_], scale=cw[:, 0, ch:ch + 1])
                    self.act(t2_[:, 0:n], uu[:, 0:n], AF.Copy, [uu, cw], [t2_], scale=cw[:, 2, ch:ch + 1])
                    self.I("dve", "tensor_tensor", [t0_, cu_], [cu_], out=cu_[:, 1:n], in0=t0_[:, 0:n - 1], in1=cu_[:, 1:n], op=ALU.add)
                    self.I("pool", "tensor_tensor", [t2_, cu_], [cu_], out=cu_[:, 0:n - 1], in0=t2_[:, 1:n], in1=cu_[:, 0:n - 1],
                           op=ALU.add)
                    self.act(sg_[:, 0:n], cg_[:, 0:n], AF.Silu, [cg_], [sg_])
                    a0 = o0 - c0
                    no = o1 - o0
                    self.I("pool", "tensor_tensor", [sg_, cu_], [a_], out=a_[:, jj, 0:no], in0=sg_[:, a0:a0 + no], in1=cu_[:, a0:a0 + no],
                           op=ALU.mult)
                    if pend_tiles and jj >= 1:
                        pend_tiles.pop(0)()
                while pend_tiles:
                    pend_tiles.pop(0)()

                def down_tile(ti, a_=a_, o0=o0, o1=o1, d_=d_, j0=j0, j1=j1, tin=tin, tnew=tnew):
                    no = o1 - o0
                    if True:
                        r0 = ti * 128
                        r1 = min(r0 + 128, no)
                        nr = r1 - r0
                        it = self._it
                        self._it += 1
                        xa_, xo_ = xa[it % 3], xo[it % 2]
                        yb = (4, 5) if it % 2 == 0 else (6, 7)
                        self.dma("sp", xa_[0:nr, :], self.out[o0 + r0:o0 + r1, :], [tin], [xa_], xa_)
                        self.mm([(bank[yb[0]][0:nr, :], [(a_[:, jj, r0:r1], d_[:, jj, 0:512]) for jj in range(j1 - j0)]),
                                 (bank[yb[1]][0:nr, :], [(a_[:, jj, r0:r1], d_[:, jj, 512:1024]) for jj in range(j1 - j0)])],
                                [a_, d_], [bank[yb[0]], bank[yb[1]]])
                        yap = self.pall[0:nr, yb[0] * 512:yb[0] * 512 + 1024]
                        self.I("dve", "tensor_tensor", [bank[yb[0]], bank[yb[1]], xa_], [xo_], out=xo_[0:nr, :], in0=yap,
                               in1=xa_[0:nr, :], op=ALU.add)
                        self.dma("sp", self.out[o0 + r0:o0 + r1, :], xo_[0:nr, :], [xo_], [tnew], xo_)
                pend_tiles = [(lambda ti=ti, f=down_tile: f(ti)) for ti in range((o1 - o0 + 127) // 128)]
            while pend_tiles:
                pend_tiles.pop(0)()
        S.barrier()
        M.release(m0)

    def build(self):
        self.setup()
        done = self.stop_after is not None and self.stop_after[1] == "setup"
        if done:
            self.S.emit()
            return self.nc
        for l in range(self.nlayers):
            for ph in ("a", "mla", "dil", "c"):
                getattr(self, "phase_" + ph)(l)
                if self.stop_after is not None and self.stop_after[0] == l and self.stop_after[1] in (ph, "c1" if ph == "c" else ph):
                    done = True
                    break
            if done:
                break
        self.S.emit()
        return self.nc


def t5_bucket_np(rel):
    nb = 16
    max_exact = 8
    ret = (rel > 0).astype(np.int32) * nb
    n = np.abs(rel)
    v = np.log(np.maximum(n, 1).astype(np.float32) / np.float32(max_exact)) / np.float32(math.log(1024 / max_exact))
    large = max_exact + (v * np.float32(nb - max_exact)).astype(np.int32)
    large = np.minimum(large, nb - 1)
    return ret + np.where(n < max_exact, n, large)


def host_prep(inputs):
    rel_bias = np.asarray(inputs["rel_bias"], np.float32)
    i = np.arange(128)[:, None, None]
    dl = np.arange(3)[None, :, None]
    j = np.arange(128)[None, None, :]
    ridx = (dl - 1) * 128 + i - j
    biasM = np.empty((3, 128, 8, 384), np.float32)
    for di, d in enumerate((1, 4, 16)):
        rel = (ridx * d).astype(np.int32)
        bk = t5_bucket_np(rel)
        vals = rel_bias[bk]
        vals = np.where((np.abs(ridx) <= 64)[..., None], vals, np.float32(-30000.0))
        biasM[di] = np.transpose(vals, (0, 3, 1, 2)).reshape(128, 8, 384)
    invf = np.power(np.float32(10000.0), -np.arange(16, dtype=np.float32) / np.float32(16)).astype(np.float32)
    common = {"biasM": np.ascontiguousarray(biasM.reshape(3, 128, 8 * 384)), "invf": invf}
    for k in ("attn_norm", "w_in", "g_cq", "g_ckv", "w_uq", "w_ukv", "g_mla_q", "g_mla_k", "g_dil_q", "g_dil_k", "w_out",
              "ffn_norm", "w_up", "w_down"):
        common[k] = np.ascontiguousarray(np.asarray(inputs[k], np.float32))
    cwh = np.asarray(inputs["conv_w"], np.float32)
    common["conv_w"] = np.ascontiguousarray(cwh.reshape(2, 3, 44, 128).transpose(0, 3, 1, 2).reshape(2, 128, 132))
    cbh = np.asarray(inputs["conv_b"], np.float32)
    common["conv_b"] = np.ascontiguousarray(cbh.reshape(2, 44, 128).transpose(0, 2, 1))
    x = np.asarray(inputs["x"], np.float32)
    pos = np.asarray(inputs["positions"], np.int32)
    in_maps = []
    for c in range(8):
        m = dict(common)
        m["x"] = np.ascontiguousarray(x[c])
        m["pos"] = np.ascontiguousarray(pos[c].reshape(32, 128).T)
        in_maps.append(m)
    return in_maps


def kernel(**inputs):
    in_maps = host_prep(inputs)
    nc = K().build()
    res = run_bass_kernel_spmd(nc, in_maps, core_ids=list(range(8)))
    return np.stack([np.asarray(r["out"], np.float32) for r in res.results], axis=0)
```

```python
import math
import numpy as np
import concourse.bass as bass
import concourse.mybir as mybir
from concourse.bass_utils import run_bass_kernel_spmd

F32 = mybir.dt.float32
BF16 = mybir.dt.bfloat16
I32 = mybir.dt.int32
ALU = mybir.AluOpType
AF = mybir.ActivationFunctionType
AX = mybir.AxisListType

S_LEN = 4096
D = 1024
NT = 32
EPS = 1e-6
TWO_PI = float(2 * np.pi)
NBIG = 105984


class T:
    def __init__(self, name, ap=None):
        self.name = name
        self.ap = ap
        self.writers = []
        self.readers = []
        self.prev = []
        self.dsem = None
        self.dcount = 0
        self.excl = False

    def __getitem__(self, k):
        return self.ap[k]


class Sched:
    ENG = ("pe", "act", "dve", "pool", "sp")

    def __init__(self, nc):
        self.nc = nc
        self.sem = {k: nc.alloc_semaphore("s_" + k) for k in ("pe", "act", "dve", "pool")}
        self.cnt = {k: 0 for k in self.sem}
        self.streams = {k: [] for k in self.ENG}
        self.known = {k: {} for k in self.ENG}
        self.semobj = {}
        self.bar = []
        self.dma_ts = []
        self.free_dsems = {"hw": [], "sw": []}
        self.nsem_alloc = 0
        self.all_dsems = {}
        self.nops = 0

    def _dsem(self, t, kind):
        if t.dsem is None:
            fl = self.free_dsems[kind]
            if fl:
                t.dsem, t.dcount = fl.pop()
            else:
                self.nsem_alloc += 1
                t.dsem = self.nc.alloc_semaphore("d%s%d" % (kind, self.nsem_alloc))
                t.dcount = 0
            t.dkind = kind
            self.dma_ts.append(t)
        assert t.dkind == kind, "tile %s gets DMAs from both HW and SW queues" % t.name
        return t.dsem

    def op(self, eng, fn, reads=(), writes=(), dma_dst=None, pwrites=()):
        deps = list(self.bar)
        excl = []
        for t in list(reads) + list(writes) + list(pwrites):
            if t.excl and t not in excl:
                excl.append(t)
        if excl:
            reads = [t for t in reads if not t.excl]
            writes = [t for t in writes if not t.excl]
            pwrites = [t for t in pwrites if not t.excl]
            for t in excl:
                deps += t.writers
        for t in reads:
            deps += t.writers
        for t in writes:
            if t.readers:
                t.prev = t.readers + t.writers
                t.readers = []
                t.writers = []
            deps += t.prev + t.writers
        for t in pwrites:
            if t.readers:
                t.prev = t.readers + t.writers
                t.readers = []
                t.writers = []
            deps += t.prev
        writes = list(writes) + list(pwrites)
        if dma_dst is not None:
            s = self._dsem(dma_dst, "sw" if eng == "pool" else "hw")
            dma_dst.dcount += 16
            tok = (s, dma_dst.dcount, 16)
            self.all_dsems[id(s)] = [s, dma_dst.dcount]
        else:
            self.cnt[eng] += 1
            tok = (self.sem[eng], self.cnt[eng], 1)
        for t in reads:
            t.readers.append(tok)
        for t in writes:
            t.writers.append(tok)
        for t in excl:
            t.writers = [tok]
        need = {}
        for (s, v, _) in deps:
            if eng == "pe" and s is self.sem["pe"]:
                continue
            k = id(s)
            self.semobj[k] = s
            if v > need.get(k, 0):
                need[k] = v
        waits = []
        kn = self.known[eng]
        for k, v in need.items():
            if kn.get(k, 0) >= v:
                continue
            kn[k] = v
            waits.append((self.semobj[k], v))
        self.streams[eng].append((waits, fn, tok))
        self.nops += 1
        return tok

    def barrier(self):
        toks = [(self.sem[k], self.cnt[k], 1) for k in self.sem if self.cnt[k] > 0]
        for k, (s, v) in self.all_dsems.items():
            toks.append((s, v, 16))
        self.bar = toks
        for t in self.dma_ts:
            self.free_dsems[t.dkind].append((t.dsem, t.dcount))
            t.dsem = None
        self.dma_ts = []

    def emit(self):
        nc = self.nc
        sched = self
        final = [(s, v) for (s, v) in self.all_dsems.values()]

        def run(e, key):
            for waits, fn, tok in sched.streams[key]:
                for (s, v) in waits:
                    e.wait_ge(s, v)
                ins = fn(e)
                ins.then_inc(tok[0], tok[2])
            if key == "sp":
                for (s, v) in final:
                    e.wait_ge(s, v)

        with nc.Block() as block:
            @block.tensor
            def _(e):
                run(e, "pe")

            @block.scalar
            def _(e):
                run(e, "act")

            @block.vector
            def _(e):
                run(e, "dve")

            @block.gpsimd
            def _(e):
                run(e, "pool")

            @block.sync
            def _(e):
                run(e, "sp")


class Mem:
    def __init__(self, nc):
        self.big = nc.alloc_sbuf_tensor("big", [128, NBIG], BF16).ap()
        self.off = 0
        self.n = 0

    def mark(self):
        return self.off

    def release(self, m):
        self.off = m

    def alloc(self, name, shape, dt):
        esz = 2 if dt == BF16 else 4
        nb = int(np.prod(shape[1:])) * esz
        off = (self.off + 31) // 32 * 32
        assert off + nb <= NBIG * 2, "SBUF overflow at %s: %d" % (name, off + nb)
        self.off = off + nb
        v = self.big[0:shape[0], off // 2:(off + nb) // 2]
        if dt != BF16:
            v = v.bitcast(dt)
        if len(shape) == 3:
            v = v.rearrange("p (a b) -> p a b", b=shape[2])
        elif len(shape) == 4:
            v = v.rearrange("p (a b c) -> p a b c", b=shape[2], c=shape[3])
        self.n += 1
        return T("%s_%d" % (name, self.n), v)

    def ring(self, name, n, shape, dt):
        return [self.alloc(name, shape, dt) for _ in range(n)]


def h3(ap, f):
    return ap.rearrange("p (h f) -> p h f", f=f)


class K:
    def __init__(self, nlayers=2, stop_after=None, dbg=False):
        self.nlayers = nlayers
        self.stop_after = stop_after
        self.dbg = dbg
        nc = self.nc = bass.Bass("TRN2", target_bir_lowering=False)
        self.S = Sched(nc)
        self.M = Mem(nc)
        L = 2

        def din(name, shape, dt=F32):
            return nc.dram_tensor(name, shape, dt, kind="ExternalInput").ap()

        self.x = din("x", [S_LEN, D])
        self.pos = din("pos", [128, 32], I32)
        self.biasM = din("biasM", [3, 128, 8 * 384])
        self.invf = din("invf", [16])
        self.attn_norm = din("attn_norm", [L, D])
        self.w_in = din("w_in", [L, D, 2208])
        self.g_cq = din("g_cq", [L, 384])
        self.g_ckv = din("g_ckv", [L, 256])
        self.w_uq = din("w_uq", [L, 384, 768])
        self.w_ukv = din("w_ukv", [L, 256, 1024])
        self.g_mla_q = din("g_mla_q", [L, 96])
        self.g_mla_k = din("g_mla_k", [L, 96])
        self.g_dil_q = din("g_dil_q", [L, 64])
        self.g_dil_k = din("g_dil_k", [L, 64])
        self.w_out = din("w_out", [L, D, D])
        self.ffn_norm = din("ffn_norm", [L, D])
        self.w_up = din("w_up", [L, D, 5632])
        self.conv_w = din("conv_w", [L, 128, 3 * 44])
        self.conv_b = din("conv_b", [L, 128, 44])
        self.w_down = din("w_down", [L, 2816, D])
        self.out = nc.dram_tensor("out", [S_LEN, D], F32, kind="ExternalOutput").ap()
        kind = "ExternalOutput" if dbg else "Internal"

        def scr(name, shape):
            return nc.dram_tensor(name, shape, BF16, kind=kind).ap()

        self.QT = scr("QT", [8, 96, S_LEN])
        self.KT = scr("KT", [8, 96, S_LEN])
        self.VA = scr("VA", [S_LEN, 520])
        self.QdT = scr("QdT", [4, 128, S_LEN])
        self.KdT = scr("KdT", [4, 128, S_LEN])
        self.VdA = scr("VdA", [S_LEN, 520])
        self.MT = scr("MT", [8, 128, S_LEN])
        self.RD = nc.dram_tensor("RD", [64, 512], F32, kind="Internal").ap()
        self.RD2 = nc.dram_tensor("RD2", [2 * S_LEN], F32, kind="Internal").ap()
        self.tRD = [T("RDa"), T("RDb")]
        self.tRD2 = T("RD2")
        self.RD3 = nc.dram_tensor("RD3", [2 * S_LEN], F32, kind="Internal").ap()
        self.tRD3 = T("RD3")
        self.tQT, self.tKT, self.tVA = T("QT"), T("KT"), T("VA")
        self.tQdT, self.tKdT, self.tVdA, self.tMT = T("QdT"), T("KdT"), T("VdA"), T("MT")
        self.tXin = T("Xin")
        self.outv = 0
        self.tOut = T("out0")
        pall = nc.alloc_psum_tensor("pall", [128, 4096], F32).ap()
        self.pall = pall
        self.bank = [T("bank%d" % i, pall[:, 512 * i:512 * (i + 1)]) for i in range(8)]
        for b_ in self.bank:
            b_.excl = True

    def new_out(self):
        self.outv += 1
        self.tOut = T("out%d" % self.outv)
        return self.tOut

    def I(self, eng, name, reads, writes, **kw):
        return self.S.op(eng, lambda e: getattr(e, name)(**kw), reads, writes)

    def act(self, out, in_, func, reads, writes, **kw):
        return self.S.op("act", lambda e: e.activation(out=out, in_=in_, func=func, **kw), reads, writes)

    def dma(self, q, out, in_, reads, writes, dst, **kw):
        return self.S.op(q, lambda e: e.dma_start(out=out, in_=in_, **kw), reads, (), dma_dst=dst, pwrites=writes)

    def mm(self, groups, reads, writes):
        def fn(e):
            ins = None
            for out_ap, pairs in groups:
                n = len(pairs)
                for i, (l, r) in enumerate(pairs):
                    ins = e.matmul(out_ap, lhsT=l, rhs=r, start=(i == 0), stop=(i == n - 1))
            return ins
        return self.S.op("pe", fn, reads, writes)

    def tr(self, items, reads, writes):
        ident = self.ident

        def fn(e):
            ins = None
            for o, i in items:
                ins = e.transpose(out=o, in_=i, identity=ident.ap[:])
            return ins
        return self.S.op("pe", fn, list(reads) + [ident], writes)

    def bank_bf(self, i):
        return self.bank[i].ap.bitcast(BF16)

    def setup(self):
        S, M, nc = self.S, self.M, self.nc
        self.ident = M.alloc("ident", [128, 128], BF16)
        self.onesf = M.alloc("onesf", [128, 64], F32)
        self.cosT = M.alloc("cosT", [128, 512], F32)
        self.sinT = M.alloc("sinT", [128, 512], F32)
        m = M.mark()
        identf = M.alloc("identf", [128, 128], F32)
        self.I("pool", "memset", [], [identf], ap=identf[:], constant=0.0)
        self.I("pool", "affine_select", [identf], [identf], out=identf[:], in_=identf[:], pattern=[[-1, 128]],
               compare_op=ALU.not_equal, fill=1.0, base=0, channel_multiplier=1)
        self.I("dve", "tensor_copy", [identf], [self.ident], out=self.ident[:], in_=identf[:])
        self.I("pool", "memset", [], [self.onesf], ap=self.onesf[:], constant=1.0)
        pi_ = M.alloc("pi", [128, 32], I32)
        pf = M.alloc("pf", [128, 32], F32)
        ivf = M.alloc("ivf", [128, 16], F32)
        ang = M.alloc("ang", [128, 512], F32)
        tq = M.alloc("tq", [128, 512], F32)
        ki = M.alloc("ki", [128, 512], I32)
        kf = M.alloc("kf", [128, 512], F32)
        r = M.alloc("r", [128, 512], F32)
        self.dma("sp", pi_[:], self.pos[:, :], [], [pi_], pi_)
        self.dma("sp", ivf[:], self.invf.partition_broadcast(128), [], [ivf], ivf)
        self.I("dve", "tensor_copy", [pi_], [pf], out=pf[:], in_=pi_[:])
        self.I("dve", "tensor_tensor", [pf, ivf], [ang], out=h3(ang[:], 16),
               in0=pf[:].unsqueeze(2).to_broadcast([128, 32, 16]),
               in1=ivf[:].unsqueeze(1).to_broadcast([128, 32, 16]), op=ALU.mult)

        def trig(dst, shift):
            self.I("dve", "tensor_scalar", [ang], [tq], out=tq[:], in0=ang[:], scalar1=shift, scalar2=1.0 / TWO_PI,
                   op0=ALU.add, op1=ALU.mult)
            self.I("dve", "tensor_copy", [tq], [ki], out=ki[:], in_=tq[:])
            self.I("dve", "tensor_copy", [ki], [kf], out=kf[:], in_=ki[:])
            self.I("dve", "scalar_tensor_tensor", [kf, ang], [r], out=r[:], in0=kf[:], scalar=-TWO_PI, in1=ang[:],
                   op0=ALU.mult, op1=ALU.add)
            self.I("dve", "tensor_scalar", [r], [r], out=r[:], in0=r[:], scalar1=shift, scalar2=None, op0=ALU.add)
            self.I("dve", "tensor_scalar", [r], [tq], out=tq[:], in0=r[:], scalar1=float(np.pi), scalar2=-TWO_PI,
                   op0=ALU.is_gt, op1=ALU.mult)
            self.I("dve", "tensor_tensor", [r, tq], [r], out=r[:], in0=r[:], in1=tq[:], op=ALU.add)
            self.act(dst[:], r[:], AF.Sin, [r], [dst], scale=0.999999)

        trig(self.sinT, 0.0)
        trig(self.cosT, float(np.pi / 2))
        S.barrier()
        M.release(m)

    def phase_a(self, l):
        S, M, nc = self.S, self.M, self.nc
        bank = self.bank
        m0 = M.mark()
        xsrc, txsrc = (self.x, self.tXin) if l == 0 else (self.out, self.tOut)
        w_in_sb = M.alloc("w_in", [128, 8, 2208], BF16)
        w_uq_sb = M.alloc("w_uq", [128, 3, 768], BF16)
        w_ukv_sb = M.alloc("w_ukv", [128, 2, 1024], BF16)
        w_in_k = [T("w_in_k%d" % k, w_in_sb.ap[:, k, :]) for k in range(8)]
        for k in range(8):
            self.dma("pool", w_in_sb[:, k, :], self.w_in[l, k * 128:(k + 1) * 128, :], [], [w_in_sb, w_in_k[k]], w_in_k[k])
        self.dma("pool", w_uq_sb[:], self.w_uq[l].rearrange("(k p) n -> p k n", p=128), [], [w_uq_sb], w_uq_sb)
        self.dma("pool", w_ukv_sb[:], self.w_ukv[l].rearrange("(k p) n -> p k n", p=128), [], [w_ukv_sb], w_ukv_sb)

        def gain(name, src, n):
            t = M.alloc(name, [128, n], F32)
            self.dma("sp", t[:], src.partition_broadcast(128), [], [t], t)
            return t
        gA = gain("gA", self.attn_norm[l], D)
        gcl = M.alloc("gcl", [128, 640], F32)
        self.dma("sp", gcl[:, 0:384], self.g_cq[l].partition_broadcast(128), [], [gcl], gcl)
        self.dma("sp", gcl[:, 384:640], self.g_ckv[l].partition_broadcast(128), [], [gcl], gcl)
        gq = gain("gq", self.g_mla_q[l], 96)
        gk = gain("gk", self.g_mla_k[l], 96)
        gdq = gain("gdq", self.g_dil_q[l], 64)
        gdk = gain("gdk", self.g_dil_k[l], 64)
        self.I("dve", "tensor_scalar", [gdk], [gdk], out=gdk[:], in0=gdk[:], scalar1=8.0, scalar2=None, op0=ALU.mult)
        self.I("dve", "tensor_tensor", [gdk, gdq], [gdk], out=gdk[:], in0=gdk[:], in1=gdq[:], op=ALU.mult)
        self.I("dve", "tensor_scalar", [gk], [gk], out=gk[:], in0=gk[:], scalar1=float(math.sqrt(96.0)), scalar2=None,
               op0=ALU.mult)
        def rope_tabs(g, nm):
            AC = M.alloc("AC" + nm, [128, 32, 32], F32)
            BD = M.alloc("BD" + nm, [128, 32, 32], F32)
            c3 = h3(self.cosT[:], 16)
            s3 = h3(self.sinT[:], 16)
            g1 = g[:, 64:80].unsqueeze(1).to_broadcast([128, 32, 16])
            g2 = g[:, 80:96].unsqueeze(1).to_broadcast([128, 32, 16])
            rd = [g, self.cosT, self.sinT]
            self.I("dve", "tensor_tensor", rd, [AC], out=AC[:, :, 0:16], in0=c3, in1=g1, op=ALU.mult)
            self.I("dve", "tensor_tensor", rd, [AC], out=AC[:, :, 16:32], in0=c3, in1=g2, op=ALU.mult)
            self.I("dve", "scalar_tensor_tensor", rd, [BD], out=BD[:, :, 0:16], in0=s3, scalar=-1.0, in1=g2,
                   op0=ALU.mult, op1=ALU.mult)
            self.I("dve", "tensor_tensor", rd, [BD], out=BD[:, :, 16:32], in0=s3, in1=g1, op=ALU.mult)
            return AC, BD
        ACq, BDq = rope_tabs(gq, "q")
        ACk, BDk = rope_tabs(gk, "k")

        xt = M.ring("xt", 4, [128, D], F32)
        junk = M.alloc("junk", [128, D], F32)
        st = M.ring("st", 2, [128, 32], F32)
        hb = M.ring("hb", 3, [128, D], BF16)
        hT = M.ring("hT", 2, [128, 8, 128], BF16)
        cln = M.ring("cln", 2, [128, 640], BF16)
        cT = M.ring("cT", 2, [128, 5, 128], BF16)
        kr = M.ring("kr", 2, [128, 32], F32)
        kru = M.ring("kru", 2, [128, 32], F32)
        krv = M.ring("krv", 2, [128, 32], F32)
        krr = M.ring("krr", 2, [128, 32], F32)
        qsq = M.alloc("qsq", [128, 768], F32)
        qn = M.alloc("qn", [128, 768], F32)
        ru = M.alloc("ru", [128, 8, 32], F32)
        rv = M.alloc("rv", [128, 8, 32], F32)
        ksq = M.alloc("ksq", [128, 512], F32)
        kn = M.alloc("kn", [128, 512], F32)
        dsq = M.ring("dsq", 2, [128, 512], F32)
        dn = M.ring("dn", 2, [128, 512], F32)
        qf = M.ring("qf", 2, [128, 768], BF16)
        kfb = M.ring("kfb", 2, [128, 768], BF16)
        qdf = M.ring("qdf", 3, [128, 512], BF16)
        kdf = M.ring("kdf", 3, [128, 512], BF16)
        QTg = M.ring("QTg", 2, [128, 8, 512], BF16)
        KTg = M.ring("KTg", 2, [128, 8, 512], BF16)
        QdTg = M.ring("QdTg", 2, [128, 4, 512], BF16)
        KdTg = M.ring("KdTg", 2, [128, 4, 512], BF16)
        VAs = M.ring("VAs", 2, [128, 8, 65], BF16)
        VdAs = M.ring("VdAs", 2, [128, 8, 65], BF16)
        for t_ in VAs + VdAs:
            self.I("pool", "memset", [], [t_], ap=t_[:], constant=1.0)
        st2 = M.ring("st2", 2, [128, 32], F32)
        TPB = 0
        tpb = self.bank_bf(TPB)
        pq = self.pall[:, 3072:3840]
        pkv = self.pall[:, 3072:4096]
        bq = [bank[6], bank[7]]

        sf = M.ring("sf", 4, [128, 2], F32)

        def front_elem(t):
            x_t, s_t, hb_t = xt[t % 4], sf[t % 4], hb[t % 3]
            if t == 0:
                for tt_ in range(3):
                    self.dma("sp", xt[tt_ % 4][:], xsrc[tt_ * 128:(tt_ + 1) * 128, :], [txsrc], [xt[tt_ % 4]], xt[tt_ % 4])
            if t + 3 < NT:
                tt_ = t + 3
                self.dma("sp", xt[tt_ % 4][:], xsrc[tt_ * 128:(tt_ + 1) * 128, :], [txsrc], [xt[tt_ % 4]], xt[tt_ % 4])
            self.act(junk[:], x_t[:], AF.Square, [x_t], [s_t, junk], accum_out=s_t[:, 0:1])
            self.act(s_t[:, 0:1], s_t[:, 0:1], AF.Sqrt, [s_t], [s_t], bias=EPS, scale=1.0 / D)
            self.I("dve", "reciprocal", [s_t], [s_t], out=s_t[:, 1:2], in_=s_t[:, 0:1])
            self.I("dve", "scalar_tensor_tensor", [x_t, s_t, gA], [hb_t], out=hb_t[:], in0=x_t[:], scalar=s_t[:, 1:2],
                   in1=gA[:], op0=ALU.mult, op1=ALU.mult)

        def front_pe(t):
            hb_t, hT_t = hb[t % 3], hT[t % 2]
            self.tr([(tpb[:, k * 128:(k + 1) * 128], hb_t[:, k * 128:(k + 1) * 128]) for k in range(8)], [hb_t], [bank[TPB]])
            self.act(hT_t[:].rearrange("p a b -> p (a b)"), tpb[:, 0:1024], AF.Copy, [bank[TPB]], [hT_t])

        def wstage(t):
            s_t, s2_t, hT_t, cln_t = st[t % 2], st2[t % 2], hT[t % 2], cln[t % 2]
            cols = [(1, 0, 384), (2, 384, 672), (3, 672, 1184), (4, 1184, 1696), (5, 1696, 2208)]
            if t == 0:
                for k in range(8):
                    def wk(e, k=k):
                        ins = None
                        for (bi, c0, c1) in cols:
                            ins = e.matmul(bank[bi][:, 0:c1 - c0], lhsT=hT_t[:, k, :], rhs=w_in_sb[:, k, c0:c1],
                                           start=(k == 0), stop=(k == 7))
                        return ins
                    self.S.op("pe", wk, [hT_t, w_in_k[k]], [bank[bi_] for (bi_, _, _) in cols])
            else:
                for (bi, c0, c1) in cols:
                    self.mm([(bank[bi][:, 0:c1 - c0], [(hT_t[:, k, :], w_in_sb[:, k, c0:c1]) for k in range(8)])],
                            [hT_t, w_in_sb], [bank[bi]])
            self.act(junk[:, 0:384], bank[1][:, 0:384], AF.Square, [bank[1]], [s_t, junk], accum_out=s_t[:, 2:3])
            self.act(junk[:, 0:256], bank[2][:, 0:256], AF.Square, [bank[2]], [s_t, junk], accum_out=s_t[:, 3:4])
            self.act(junk[:, 0:32], bank[2][:, 256:288], AF.Square, [bank[2]], [s_t, junk], accum_out=s_t[:, 6:7])
            self.act(s_t[:, 2:3], s_t[:, 2:3], AF.Sqrt, [s_t], [s_t], bias=EPS, scale=1.0 / 384)
            self.act(s_t[:, 3:4], s_t[:, 3:4], AF.Sqrt, [s_t], [s_t], bias=EPS, scale=1.0 / 256)
            self.I("dve", "reciprocal", [s_t], [s_t], out=s_t[:, 4:6], in_=s_t[:, 2:4])
            self.I("dve", "tensor_scalar", [s_t], [s_t], out=s_t[:, 7:8], in0=s_t[:, 6:7], scalar1=96 * EPS, scalar2=None,
                   op0=ALU.add)
            self.I("dve", "scalar_tensor_tensor", [bank[1], s_t, gcl], [cln_t], out=cln_t[:, 0:384], in0=bank[1][:, 0:384],
                   scalar=s_t[:, 4:5], in1=gcl[:, 0:384], op0=ALU.mult, op1=ALU.mult)
            self.I("dve", "scalar_tensor_tensor", [bank[2], s_t, gcl], [cln_t], out=cln_t[:, 384:640],
                   in0=bank[2][:, 0:256], scalar=s_t[:, 5:6], in1=gcl[:, 384:640], op0=ALU.mult, op1=ALU.mult)
            kr_t, kru_t, krv_t, krr_t = kr[t % 2], kru[t % 2], krv[t % 2], krr[t % 2]
            self.I("dve", "tensor_copy", [bank[2]], [kr_t], out=kr_t[:], in_=bank[2][:, 256:288])
            self.I("pool", "tensor_tensor", [kr_t, ACk], [kru_t], out=kru_t[:], in0=kr_t[:], in1=ACk[:, t, :], op=ALU.mult)
            self.I("pool", "tensor_tensor", [kr_t, BDk], [krv_t], out=krv_t[:, 0:16], in0=kr_t[:, 16:32], in1=BDk[:, t, 0:16],
                   op=ALU.mult)
            self.I("pool", "tensor_tensor", [kr_t, BDk], [krv_t], out=krv_t[:, 16:32], in0=kr_t[:, 0:16],
                   in1=BDk[:, t, 16:32], op=ALU.mult)
            self.I("pool", "tensor_tensor", [kru_t, krv_t], [krr_t], out=krr_t[:], in0=kru_t[:], in1=krv_t[:], op=ALU.add)
            for (bi, which) in ((3, 0), (4, 1)):
                dsq_t, dn_t = dsq[which], dn[which]
                gt = gdq if which == 0 else gdk
                dst = qdf[t % 3] if which == 0 else kdf[t % 3]
                so = 8 * which
                self.act(dsq_t[:], bank[bi][:, 0:512], AF.Square, [bank[bi]], [dsq_t])
                self.I("dve", "tensor_reduce", [dsq_t], [s2_t], out=s2_t[:, so:so + 8], in_=h3(dsq_t[:], 64), axis=AX.X,
                       op=ALU.add)
                self.act(s2_t[:, so:so + 8], s2_t[:, so:so + 8], AF.Sqrt, [s2_t], [s2_t], bias=64 * EPS, scale=1.0)
                self.I("dve", "reciprocal", [s2_t], [s2_t], out=s2_t[:, 16 + so:24 + so], in_=s2_t[:, so:so + 8])
                if which == 0:
                    self.I("dve", "tensor_tensor", [bank[bi], s2_t], [dst], out=h3(dst[:], 64), in0=h3(bank[bi][:, 0:512], 64),
                           in1=s2_t[:, 16 + so:24 + so].unsqueeze(2).to_broadcast([128, 8, 64]), op=ALU.mult)
                else:
                    self.I("dve", "tensor_tensor", [bank[bi], s2_t], [dn_t], out=h3(dn_t[:], 64), in0=h3(bank[bi][:, 0:512], 64),
                           in1=s2_t[:, 16 + so:24 + so].unsqueeze(2).to_broadcast([128, 8, 64]), op=ALU.mult)
                    self.I("pool", "tensor_tensor", [dn_t, gt], [dst], out=h3(dst[:], 64), in0=h3(dn_t[:], 64),
                           in1=gt[:].unsqueeze(1).to_broadcast([128, 8, 64]), op=ALU.mult)
            VdA_t = VdAs[t % 2]
            self.act(VdA_t[:, :, 0:64], h3(bank[5][:, 0:512], 64), AF.Copy, [bank[5]], [VdA_t])
            self.dma("sp", self.VdA[t * 128:(t + 1) * 128, :], VdA_t[:].rearrange("p a b -> p (a b)"), [VdA_t], [self.tVdA],
                     VdA_t)

        def t2stage(t):
            cln_t, cT_t = cln[t % 2], cT[t % 2]
            self.tr([(tpb[:, k * 128:(k + 1) * 128], cln_t[:, k * 128:(k + 1) * 128]) for k in range(5)], [cln_t], [bank[TPB]])
            self.I("dve", "tensor_copy", [bank[TPB]], [cT_t], out=cT_t[:].rearrange("p a b -> p (a b)"), in_=tpb[:, 0:640])

        def qstage(t):
            s_t, cT_t, qf_t = st[t % 2], cT[t % 2], qf[t % 2]
            self.mm([(self.pall[:, 3072:3584], [(cT_t[:, k, :], w_uq_sb[:, k, 0:512]) for k in range(3)]),
                     (self.pall[:, 3584:3840], [(cT_t[:, k, :], w_uq_sb[:, k, 512:768]) for k in range(3)])],
                    [cT_t, w_uq_sb], bq)
            self.act(qsq[:], pq, AF.Square, bq, [qsq])
            self.I("dve", "tensor_reduce", [qsq], [s_t], out=s_t[:, 8:16], in_=h3(qsq[:], 96), axis=AX.X, op=ALU.add)
            self.act(s_t[:, 8:16], s_t[:, 8:16], AF.Sqrt, [s_t], [s_t], bias=96 * EPS, scale=1.0)
            self.I("dve", "reciprocal", [s_t], [s_t], out=s_t[:, 8:16], in_=s_t[:, 8:16])
            self.I("dve", "tensor_tensor", bq + [s_t], [qn], out=h3(qn[:], 96), in0=h3(pq, 96),
                   in1=s_t[:, 8:16].unsqueeze(2).to_broadcast([128, 8, 96]), op=ALU.mult)
            qn3 = h3(qn[:], 96)
            qf3 = h3(qf_t[:], 96)
            self.I("pool", "tensor_tensor", [qn, gq], [qf_t], out=qf3[:, :, 0:64], in0=qn3[:, :, 0:64],
                   in1=gq[:, 0:64].unsqueeze(1).to_broadcast([128, 8, 64]), op=ALU.mult)
            self.I("pool", "tensor_tensor", [qn, ACq], [ru], out=ru[:], in0=qn3[:, :, 64:96],
                   in1=ACq[:, t, :].unsqueeze(1).to_broadcast([128, 8, 32]), op=ALU.mult)
            self.I("pool", "tensor_tensor", [qn, BDq], [rv], out=rv[:, :, 0:16], in0=qn3[:, :, 80:96],
                   in1=BDq[:, t, 0:16].unsqueeze(1).to_broadcast([128, 8, 16]), op=ALU.mult)
            self.I("pool", "tensor_tensor", [qn, BDq], [rv], out=rv[:, :, 16:32], in0=qn3[:, :, 64:80],
                   in1=BDq[:, t, 16:32].unsqueeze(1).to_broadcast([128, 8, 16]), op=ALU.mult)
            self.I("dve", "tensor_tensor", [ru, rv], [qf_t], out=qf3[:, :, 64:96], in0=ru[:], in1=rv[:], op=ALU.add)

        def kvstage(t):
            s_t, cT_t, kf_t, krr_t = st[t % 2], cT[t % 2], kfb[t % 2], krr[t % 2]
            self.mm([(self.pall[:, 3072:3584], [(cT_t[:, 3 + k, :], w_ukv_sb[:, k, 0:512]) for k in range(2)]),
                     (self.pall[:, 3584:4096], [(cT_t[:, 3 + k, :], w_ukv_sb[:, k, 512:1024]) for k in range(2)])],
                    [cT_t, w_ukv_sb], bq)
            kv3 = h3(pkv, 128)
            self.act(h3(ksq[:], 64), kv3[:, :, 0:64], AF.Square, bq, [ksq])
            self.I("dve", "tensor_reduce", [ksq], [s_t], out=s_t[:, 16:24], in_=h3(ksq[:], 64), axis=AX.X, op=ALU.add)
            self.act(s_t[:, 16:24], s_t[:, 16:24], AF.Sqrt, [s_t], [s_t], bias=s_t[:, 7:8], scale=1.0)
            self.I("dve", "reciprocal", [s_t], [s_t], out=s_t[:, 16:24], in_=s_t[:, 16:24])
            self.I("dve", "tensor_tensor", bq + [s_t], [kn], out=h3(kn[:], 64), in0=kv3[:, :, 0:64],
                   in1=s_t[:, 16:24].unsqueeze(2).to_broadcast([128, 8, 64]), op=ALU.mult)
            kf3 = h3(kf_t[:], 96)
            self.I("pool", "tensor_tensor", [kn, gk], [kf_t], out=kf3[:, :, 0:64], in0=h3(kn[:], 64),
                   in1=gk[:, 0:64].unsqueeze(1).to_broadcast([128, 8, 64]), op=ALU.mult)
            self.I("pool", "tensor_tensor", [krr_t, s_t], [kf_t], out=kf3[:, :, 64:96],
                   in0=krr_t[:].unsqueeze(1).to_broadcast([128, 8, 32]),
                   in1=s_t[:, 16:24].unsqueeze(2).to_broadcast([128, 8, 32]), op=ALU.mult)
            VA_t = VAs[t % 2]
            self.act(VA_t[:, :, 0:64], kv3[:, :, 64:128], AF.Copy, bq, [VA_t])
            self.dma("sp", self.VA[t * 128:(t + 1) * 128, :], VA_t[:].rearrange("p a b -> p (a b)"), [VA_t], [self.tVA], VA_t)

        def t3q(t):
            g, gi = t // 4, t % 4
            gr, c0 = g % 2, (t % 4) * 128
            qf3 = h3(qf[t % 2][:], 96)
            self.tr([(tpb[0:96, h * 128:(h + 1) * 128], qf3[:, h, :]) for h in range(8)], [qf[t % 2]], [bank[TPB]])
            self.act(QTg[gr][0:96, :, c0:c0 + 128], h3(tpb[0:96, 0:1024], 128), AF.Copy, [bank[TPB]], [QTg[gr]])

        def t3k(t):
            g, gi = t // 4, t % 4
            gr, c0 = g % 2, (t % 4) * 128
            kf3 = h3(kfb[t % 2][:], 96)
            self.tr([(tpb[0:96, h * 128:(h + 1) * 128], kf3[:, h, :]) for h in range(8)], [kfb[t % 2]], [bank[TPB]])
            self.I("dve", "tensor_copy", [bank[TPB]], [KTg[gr]], out=KTg[gr][0:96, :, c0:c0 + 128], in_=h3(tpb[0:96, 0:1024], 128))

        def t3d(t):
            g, gi = t // 4, t % 4
            gr, c0 = g % 2, (t % 4) * 128
            qd_, kd_ = qdf[t % 3], kdf[t % 3]
            self.tr([(tpb[:, p * 128:(p + 1) * 128], qd_[:, p * 128:(p + 1) * 128]) for p in range(4)] +
                    [(tpb[:, 512 + p * 128:512 + (p + 1) * 128], kd_[:, p * 128:(p + 1) * 128]) for p in range(4)],
                    [qd_, kd_], [bank[TPB]])
            self.act(QdTg[gr][:, :, c0:c0 + 128], h3(tpb[:, 0:512], 128), AF.Copy, [bank[TPB]], [QdTg[gr]])
            self.I("dve", "tensor_copy", [bank[TPB]], [KdTg[gr]], out=KdTg[gr][:, :, c0:c0 + 128], in_=h3(tpb[:, 512:1024], 128))
            if gi == 3:
                tc = slice(g * 512, (g + 1) * 512)
                self.dma("sp", self.QT[:, :, tc].rearrange("h f t -> f h t"), QTg[gr][0:96], [QTg[gr]], [self.tQT], QTg[gr])
                self.dma("sp", self.KT[:, :, tc].rearrange("h f t -> f h t"), KTg[gr][0:96], [KTg[gr]], [self.tKT], KTg[gr])
                self.dma("sp", self.QdT[:, :, tc].rearrange("h f t -> f h t"), QdTg[gr][:], [QdTg[gr]], [self.tQdT], QdTg[gr])
                self.dma("sp", self.KdT[:, :, tc].rearrange("h f t -> f h t"), KdTg[gr][:], [KdTg[gr]], [self.tKdT], KdTg[gr])

        front_elem(0)
        front_elem(1)
        front_pe(0)
        wstage(0)
        for i in range(NT):
            if i + 1 < NT:
                front_pe(i + 1)
            if i >= 1:
                t3q(i - 1)
                t3k(i - 1)
                t3d(i - 1)
            t2stage(i)
            qstage(i)
            if i + 2 < NT:
                front_elem(i + 2)
            if i + 1 < NT:
                wstage(i + 1)
            kvstage(i)
        t3q(NT - 1)
        t3k(NT - 1)
        t3d(NT - 1)
        S.barrier()
        M.release(m0)

    def phase_mla(self, l):
        S, M, nc = self.S, self.M, self.nc
        bank = self.bank
        m0 = M.mark()
        VAsb = M.alloc("VAsb", [128, 32, 520], BF16)
        QTh = M.ring("QTh", 2, [96, S_LEN], BF16)
        KTh = M.ring("KTh", 2, [96, S_LEN], BF16)
        P = M.ring("P", 3, [128, 1024], BF16)
        rden = M.ring("rden", 2, [128, 512], F32)
        Osb = M.ring("Osb", 2, [64, 512], F32)
        mix = M.ring("mix", 2, [64, 512], BF16)
        steps = [(h, qb, k2) for h in range(8) for qb in range(8) for k2 in range(16)]
        nsteps = len(steps)
        SB = [(0, 1), (2, 3), (4, 5)]
        LA = 2
        rbc = M.ring("rbc", 2, [64, 512], F32)
        deferred = {}

        def load_head(h):
            self.dma("sp", QTh[h % 2][:], self.QT[h], [self.tQT], [QTh[h % 2]], QTh[h % 2])
            self.dma("sp", KTh[h % 2][:], self.KT[h], [self.tKT], [KTh[h % 2]], KTh[h % 2])

        def emit_s(i):
            h, qb, k2 = steps[i]
            sb2 = SB[i % 3]
            q_ = QTh[h % 2][:, qb * 512:(qb + 1) * 512]
            self.mm([(bank[sb2[j]][:, :], [(KTh[h % 2][:, (2 * k2 + j) * 128:(2 * k2 + j + 1) * 128], q_)]) for j in range(2)],
                    [KTh[h % 2], QTh[h % 2]], [bank[sb2[0]], bank[sb2[1]]])

        load_head(0)
        self.dma("sp", VAsb[:], self.VA.rearrange("(t p) f -> p t f", p=128), [self.tVA], [VAsb], VAsb)
        for i in range(LA):
            emit_s(i)
        for i in range(nsteps):
            h, qb, k2 = steps[i]
            blk = h * 8 + qb
            if qb == 0 and k2 == 0 and h + 1 < 8:
                load_head(h + 1)
            sb2 = SB[i % 3]
            p_ = P[i % 3]
            ob = bank[6 + blk % 2]
            self.act(p_[:], self.pall[:, sb2[0] * 512:sb2[0] * 512 + 1024], AF.Exp, [bank[sb2[0]], bank[sb2[1]]], [p_])
            if i + LA < nsteps:
                emit_s(i + LA)

            def pv(e, ob=ob, k2=k2, h=h, p_=p_):
                e.matmul(ob[0:65, :], lhsT=VAsb[:, 2 * k2, h * 65:(h + 1) * 65], rhs=p_[:, 0:512], start=(k2 == 0), stop=False)
                return e.matmul(ob[0:65, :], lhsT=VAsb[:, 2 * k2 + 1, h * 65:(h + 1) * 65], rhs=p_[:, 512:1024], start=False,
                                stop=(k2 == 15))
            self.S.op("pe", pv, [VAsb, p_], [ob])
            if k2 == 15:
                rd, os_, mx, rb = rden[blk % 2], Osb[blk % 2], mix[blk % 2], rbc[blk % 2]
                trd = self.tRD[blk % 2]
                self.I("dve", "reciprocal", [ob], [rd], out=rd[64:65, :], in_=ob[64:65, :])
                self.I("dve", "tensor_copy", [ob], [os_], out=os_[:], in_=ob[0:64, :])
                self.dma("sp", self.RD[blk:blk + 1, :], rd[64:65, :], [rd], [trd], rd)

                def epi(os_=os_, mx=mx, rb=rb, trd=trd, blk=blk, h=h, qb=qb):
                    self.dma("sp", rb[:], self.RD[blk].partition_broadcast(64), [trd], [rb], rb)
                    self.I("pool", "tensor_tensor", [os_, rb], [mx], out=mx[:], in0=os_[:], in1=rb[:], op=ALU.mult)
                    self.dma("sp", self.MT[h // 2, (h % 2) * 64:(h % 2) * 64 + 64, qb * 512:(qb + 1) * 512], mx[:], [mx],
                             [self.tMT], mx)
                deferred[i + 2] = epi
            if i in deferred:
                deferred.pop(i)()
        for k_ in sorted(deferred):
            deferred[k_]()
        S.barrier()
        M.release(m0)

    def phase_dil(self, l):
        S, M, nc = self.S, self.M, self.nc
        bank = self.bank
        m0 = M.mark()
        maskb = M.alloc("maskb", [128, 24, 384], BF16)
        rbc = M.alloc("rbcd", [64, 2 * S_LEN], F32)
        mixb = M.ring("mixb", 2, [64, S_LEN], BF16)
        mm_ = M.mark()
        mtmp = M.ring("mtmp", 2, [128, 8 * 384], F32)
        for d_ in range(3):
            mt = mtmp[d_ % 2]
            self.dma("sp", mt[:], self.biasM[d_], [], [mt], mt)
            self.act(maskb[:, d_ * 8:(d_ + 1) * 8, :].rearrange("p a b -> p (a b)"), mt[:], AF.Exp, [mt], [maskb])
        S.barrier()
        M.release(mm_)
        Qp = M.ring("Qp", 2, [128, S_LEN], BF16)
        Kp = M.ring("Kp", 2, [128, S_LEN], BF16)
        Vp = M.ring("Vp", 2, [128, 32, 130], BF16)
        Oaccs = M.ring("Oacc", 2, [65, 2, S_LEN], F32)
        dsq_ = M.alloc("dsq_", [128, 64], F32)
        Pd = M.ring("Pd", 3, [128, 2, 384], BF16)
        Pm = M.ring("Pm", 3, [128, 2, 384], BF16)
        DIL = (1, 4, 16)
        SB = [(0, 1), (2, 3), (4, 5)]
        LA = 2
        vload = 0
        for p in range(4):
            Oacc = Oaccs[p % 2]
            self.dma("sp", Qp[p % 2][:], self.QdT[p], [self.tQdT], [Qp[p % 2]], Qp[p % 2])
            self.dma("sp", Kp[p % 2][:], self.KdT[p], [self.tKdT], [Kp[p % 2]], Kp[p % 2])
            Q_, K_ = Qp[p % 2], Kp[p % 2]
            for di, d in enumerate(DIL):
                nb = 32 // d
                V_ = Vp[vload % 2]
                vload += 1
                vsrc = self.VdA.rearrange("(tt pp c) f -> c pp tt f", pp=128, c=d)
                for c in range(d):
                    self.dma("sp", V_[:, c * nb:(c + 1) * nb, :], vsrc[c][:, :, p * 130:(p + 1) * 130], [self.tVdA], [V_], V_)

                def tok_slice(blk, nb=nb, d=d):
                    c = blk // nb
                    j0 = (blk % nb) * 128
                    s0 = c + d * j0
                    return slice(s0, s0 + d * 127 + 1, d)

                def dl_range(qb, nb=nb):
                    j = qb % nb
                    return (0 if j > 0 else 1), (2 if j < nb - 1 else 1)

                def emit_s(qb, K_=K_, Q_=Q_, tok_slice=tok_slice, dl_range=dl_range):
                    sb2 = SB[qb % 3]
                    d0, d1 = dl_range(qb)
                    groups = []
                    for dl in range(d0, d1 + 1):
                        kb = qb + dl - 1
                        for hh in range(2):
                            pr = slice(64 * hh, 64 * hh + 64)
                            groups.append((bank[sb2[hh]][:, dl * 128:(dl + 1) * 128],
                                           [(K_[pr, tok_slice(kb)], Q_[pr, tok_slice(qb)])]))
                    self.mm(groups, [K_, Q_], [bank[sb2[0]], bank[sb2[1]]])

                for qb in range(LA):
                    emit_s(qb)
                pend = None
                for qb in range(32):
                    sb2 = SB[qb % 3]
                    pd, pm = Pd[qb % 3], Pm[qb % 3]
                    ob = bank[6 + qb % 2]
                    d0, d1 = dl_range(qb)
                    cs = slice(d0 * 128, (d1 + 1) * 128)
                    s2 = self.pall[:, sb2[0] * 512:sb2[0] * 512 + 1024].rearrange("p (a b) -> p a b", b=512)[:, :, cs]
                    self.act(pd[:, :, cs], s2, AF.Exp, [bank[sb2[0]], bank[sb2[1]]], [pd])
                    self.I("dve", "tensor_tensor", [pd, maskb], [pm], out=pm[:, :, cs], in0=pd[:, :, cs],
                           in1=maskb[:, di * 8 + 2 * p:di * 8 + 2 * p + 2, cs], op=ALU.mult)
                    if qb + LA < 32:
                        emit_s(qb + LA)
                    groups = []
                    for hh in range(2):
                        pairs = []
                        for dl in range(d0, d1 + 1):
                            kb = qb + dl - 1
                            pairs.append((V_[:, kb, hh * 65:(hh + 1) * 65], pm[:, hh, dl * 128:(dl + 1) * 128]))
                        groups.append((ob[0:65, hh * 128:(hh + 1) * 128], pairs))
                    self.mm(groups, [V_, pm], [ob])
                    if pend is not None:
                        pend()

                    def pend(ob=ob, ts=tok_slice(qb), di=di, Oacc=Oacc):
                        src = ob[0:65, 0:256].rearrange("p (a b) -> p a b", b=128)
                        if di == 0:
                            self.I("dve", "tensor_copy", [ob], [Oacc], out=Oacc[:, :, ts], in_=src)
                        else:
                            self.I("dve", "tensor_tensor", [ob, Oacc], [Oacc], out=Oacc[:, :, ts], in0=src, in1=Oacc[:, :, ts],
                                   op=ALU.add)
                pend()
            oa = Oacc
            self.dma("sp", self.RD2.rearrange("(o n) -> o n", o=1), oa[64:65, :, :].rearrange("p a b -> p (a b)"), [oa],
                     [self.tRD2], oa)
            self.dma("sp", dsq_[:], self.RD2.rearrange("(p n) -> p n", p=128), [self.tRD2], [dsq_], dsq_)
            self.I("dve", "reciprocal", [dsq_], [dsq_], out=dsq_[:], in_=dsq_[:])
            self.dma("sp", self.RD3.rearrange("(p n) -> p n", p=128), dsq_[:], [dsq_], [self.tRD3], dsq_)
            self.dma("sp", rbc[:], self.RD3.partition_broadcast(64), [self.tRD3], [rbc], rbc)
            for hh in range(2):
                mx = mixb[hh]
                eng = "pool" if hh == 0 else "dve"
                self.I(eng, "tensor_tensor", [oa, rbc], [mx], out=mx[:], in0=oa[0:64, hh, :],
                       in1=rbc[:, hh * S_LEN:(hh + 1) * S_LEN], op=ALU.mult)
                self.dma("sp", self.MT[4 + p, hh * 64:hh * 64 + 64, :], mx[:], [mx], [self.tMT], mx)
        S.barrier()
        M.release(m0)

    def phase_c(self, l):
        S, M, nc = self.S, self.M, self.nc
        bank = self.bank
        m0 = M.mark()
        xsrc, txsrc = (self.x, self.tXin) if l == 0 else (self.out, self.tOut)
        h2T = M.alloc("h2T", [128, 8, S_LEN], BF16)
        NQ = [(0, 6), (6, 12), (12, 17), (17, 22)]
        wu = M.ring("wu", 2, [128, 8, 2, 768], BF16)
        wd = M.ring("wd", 2, [128, 6, D], BF16)
        cw = M.alloc("cw", [128, 3, 44], F32)
        cb = M.alloc("cb", [128, 44], F32)
        self.dma("sp", cw[:].rearrange("p a b -> p (a b)"), self.conv_w[l], [], [cw], cw)
        self.dma("sp", cb[:], self.conv_b[l], [], [cb], cb)

        def load_q(qi):
            j0, j1 = NQ[qi]
            n = (j1 - j0) * 128
            w_, d_ = wu[qi % 2], wd[qi % 2]
            src = self.w_up[l].rearrange("(k p) n -> p k n", p=128)
            self.dma("pool", w_[:, :, 0, 0:n], src[:, :, j0 * 128:j1 * 128], [], [w_], w_)
            self.dma("pool", w_[:, :, 1, 0:n], src[:, :, 2816 + j0 * 128:2816 + j1 * 128], [], [w_], w_)
            self.dma("pool", d_[:, 0:j1 - j0, :], self.w_down[l, j0 * 128:j1 * 128, :].rearrange("(j p) n -> p j n", p=128),
                     [], [d_], d_)
        m1 = M.mark()
        w_out_sb = M.alloc("w_out", [128, 8, D], BF16)
        self.dma("pool", w_out_sb[:], self.w_out[l].rearrange("(k p) n -> p k n", p=128), [], [w_out_sb], w_out_sb)
        load_q(0)
        gF = M.alloc("gF", [128, D], F32)
        self.dma("sp", gF[:], self.ffn_norm[l].partition_broadcast(128), [], [gF], gF)
        MTg = M.ring("MTg", 2, [128, 8, 512], BF16)
        xt = M.ring("xtc", 3, [128, D], F32)
        x1 = M.ring("x1", 2, [128, D], F32)
        junk = M.alloc("junkc", [128, D], F32)
        st = M.ring("stc", 2, [128, 4], F32)
        hb = M.ring("hbc", 2, [128, D], BF16)
        tOutNew = self.new_out() if l > 0 else self.tOut
        told = txsrc
        def c1_front(t):
            g, gi = t // 4, t % 4
            mt = MTg[g % 2]
            if gi == 0:
                for g_ in ([0, 1] if g == 0 else [g + 1]):
                    if g_ < 8:
                        m_ = MTg[g_ % 2]
                        self.dma("sp", m_[:], self.MT[:, :, g_ * 512:(g_ + 1) * 512].rearrange("k p t -> p k t"), [self.tMT], [m_], m_)
            x_t, x1_t, s_t, hb_t = xt[t % 3], x1[t % 2], st[t % 2], hb[t % 2]
            if t == 0:
                for tt_ in range(2):
                    self.dma("sp", xt[tt_ % 3][:], xsrc[tt_ * 128:(tt_ + 1) * 128, :], [told], [xt[tt_ % 3]], xt[tt_ % 3])
            if t + 2 < NT:
                tt_ = t + 2
                self.dma("sp", xt[tt_ % 3][:], xsrc[tt_ * 128:(tt_ + 1) * 128, :], [told], [xt[tt_ % 3]], xt[tt_ % 3])
            yb = (1, 2) if t % 2 == 0 else (3, 4)
            yap = self.pall[:, yb[0] * 512:yb[0] * 512 + 1024]
            self.mm([(bank[yb[0]][:, :], [(mt[:, k, gi * 128:(gi + 1) * 128], w_out_sb[:, k, 0:512]) for k in range(8)]),
                     (bank[yb[1]][:, :], [(mt[:, k, gi * 128:(gi + 1) * 128], w_out_sb[:, k, 512:1024]) for k in range(8)])],
                    [mt, w_out_sb], [bank[yb[0]], bank[yb[1]]])
            self.I("dve", "tensor_tensor", [bank[yb[0]], bank[yb[1]], x_t], [x1_t], out=x1_t[:], in0=yap, in1=x_t[:], op=ALU.add)
            self.dma("sp", self.out[t * 128:(t + 1) * 128, :], x1_t[:], [x1_t], [tOutNew], x1_t)
            self.act(junk[:], x1_t[:], AF.Square, [x1_t], [s_t, junk], accum_out=s_t[:, 0:1])
            self.act(s_t[:, 0:1], s_t[:, 0:1], AF.Sqrt, [s_t], [s_t], bias=EPS, scale=1.0 / D)
            self.I("dve", "reciprocal", [s_t], [s_t], out=s_t[:, 1:2], in_=s_t[:, 0:1])
            self.I("dve", "scalar_tensor_tensor", [x1_t, s_t, gF], [hb_t], out=hb_t[:], in0=x1_t[:], scalar=s_t[:, 1:2],
                   in1=gF[:], op0=ALU.mult, op1=ALU.mult)

        def c1_back(t):
            hb_t = hb[t % 2]
            b = 0 if t % 2 == 0 else 7
            tpb = self.bank_bf(b)
            self.tr([(tpb[:, k * 128:(k + 1) * 128], hb_t[:, k * 128:(k + 1) * 128]) for k in range(8)], [hb_t], [bank[b]])
            self.act(h2T[:, :, t * 128:(t + 1) * 128], h3(tpb[:, 0:1024], 128), AF.Copy, [bank[b]], [h2T])

        c1_front(0)
        for t in range(NT):
            if t + 1 < NT:
                c1_front(t + 1)
            c1_back(t)
        self.tOut = tOutNew
        S.barrier()
        M.release(m1)
        if self.stop_after is not None and self.stop_after == (l, "c1"):
            M.release(m0)
            return
        aT = M.ring("aT", 2, [128, 6, 512], BF16)
        cg = M.ring("cg", 2, [128, 512], F32)
        cu = M.ring("cu", 2, [128, 512], F32)
        sg = M.ring("sg", 2, [128, 512], F32)
        xa = M.ring("xa", 3, [128, D], F32)
        xo = M.ring("xo", 2, [128, D], F32)
        GSZ = 510
        ngroups = (S_LEN + GSZ - 1) // GSZ
        self._it = 0
        ia = 0
        pend_tiles = []
        t0 = M.ring("t0", 2, [128, 512], F32)
        t2 = M.ring("t2", 2, [128, 512], F32)
        for qi, (j0, j1) in enumerate(NQ):
            if qi + 1 < len(NQ):
                load_q(qi + 1)
            w_, d_ = wu[qi % 2], wd[qi % 2]
            tin = self.tOut
            tnew = self.new_out()
            for g in range(ngroups):
                o0 = g * GSZ
                o1 = min(o0 + GSZ, S_LEN)
                c0 = max(o0 - 1, 0)
                c1 = min(o1 + 1, S_LEN)
                n = c1 - c0
                a_ = aT[ia % 2]
                ia += 1
                for j in range(j0, j1):
                    jj = j - j0
                    ug, uu = (bank[0], bank[1]) if j % 2 == 0 else (bank[2], bank[3])
                    self.mm([(ug[:, 0:n], [(w_[:, k, 0, jj * 128:(jj + 1) * 128], h2T[:, k, c0:c1]) for k in range(8)])],
                            [w_, h2T], [ug])
                    self.mm([(uu[:, 0:n], [(w_[:, k, 1, jj * 128:(jj + 1) * 128], h2T[:, k, c0:c1]) for k in range(8)])],
                            [w_, h2T], [uu])
                    cg_, cu_, sg_, t0_, t2_ = cg[j % 2], cu[j % 2], sg[j % 2], t0[j % 2], t2[j % 2]
                    ch = j
                    self.I("dve", "tensor_scalar", [ug, cw, cb], [cg_], out=cg_[:, 0:n], in0=ug[:, 0:n], scalar1=cw[:, 1, ch:ch + 1],
                           scalar2=cb[:, ch:ch + 1], op0=ALU.mult, op1=ALU.add)
                    self.I("dve", "scalar_tensor_tensor", [ug, cw, cg_], [cg_], out=cg_[:, 1:n], in0=ug[:, 0:n - 1],
                           scalar=cw[:, 0, ch:ch + 1], in1=cg_[:, 1:n], op0=ALU.mult, op1=ALU.add)
                    self.I("dve", "scalar_tensor_tensor", [ug, cw, cg_], [cg_], out=cg_[:, 0:n - 1], in0=ug[:, 1:n],
                           scalar=cw[:, 2, ch:ch + 1], in1=cg_[:, 0:n - 1], op0=ALU.mult, op1=ALU.add)
                    ch = 22 + j
                    self.act(cu_[:, 0:n], uu[:, 0:n], AF.Identity, [uu, cw, cb], [cu_], scale=cw[:, 1, ch:ch + 1],
                             bias=cb[:, ch:ch + 1])
                    self.act(t0_[:, 0:n], uu[:, 0:n], AF.Copy, [uu, cw], [t0_], scale=cw[:, 0, ch:ch + 1])
                    self.act(t2_[:, 0:n], uu[:, 0:n], AF.Copy, [uu, cw], [t2_], scale=cw[:, 2, ch:ch + 1])
                    self.I("dve", "tensor_tensor", [t0_, cu_], [cu_], out=cu_[:, 1:n], in0=t0_[:, 0:n - 1], in1=cu_[:, 1:n], op=ALU.add)
                    self.I("pool", "tensor_tensor", [t2_, cu_], [cu_], out=cu_[:, 0:n - 1], in0=t2_[:, 1:n], in1=cu_[:, 0:n - 1],
                           op=ALU.add)
                    self.act(sg_[:, 0:n], cg_[:, 0:n], AF.Silu, [cg_], [sg_])
                    a0 = o0 - c0
                    no = o1 - o0
                    self.I("pool", "tensor_tensor", [sg_, cu_], [a_], out=a_[:, jj, 0:no], in0=sg_[:, a0:a0 + no], in1=cu_[:, a0:a0 + no],
                           op=ALU.mult)
                    if pend_tiles and jj >= 1:
                        pend_tiles.pop(0)()
                while pend_tiles:
                    pend_tiles.pop(0)()

                def down_tile(ti, a_=a_, o0=o0, o1=o1, d_=d_, j0=j0, j1=j1, tin=tin, tnew=tnew):
                    no = o1 - o0
                    if True:
                        r0 = ti * 128
                        r1 = min(r0 + 128, no)
                        nr = r1 - r0
                        it = self._it
                        self._it += 1
                        xa_, xo_ = xa[it % 3], xo[it % 2]
                        yb = (4, 5) if it % 2 == 0 else (6, 7)
                        self.dma("sp", xa_[0:nr, :], self.out[o0 + r0:o0 + r1, :], [tin], [xa_], xa_)
                        self.mm([(bank[yb[0]][0:nr, :], [(a_[:, jj, r0:r1], d_[:, jj, 0:512]) for jj in range(j1 - j0)]),
                                 (bank[yb[1]][0:nr, :], [(a_[:, jj, r0:r1], d_[:, jj, 512:1024]) for jj in range(j1 - j0)])],
                                [a_, d_], [bank[yb[0]], bank[yb[1]]])
                        yap = self.pall[0:nr, yb[0] * 512:yb[0] * 512 + 1024]
                        self.I("dve", "tensor_tensor", [bank[yb[0]], bank[yb[1]], xa_], [xo_], out=xo_[0:nr, :], in0=yap,
                               in1=xa_[0:nr, :], op=ALU.add)
                        self.dma("sp", self.out[o0 + r0:o0 + r1, :], xo_[0:nr, :], [xo_], [tnew], xo_)
                pend_tiles = [(lambda ti=ti, f=down_tile: f(ti)) for ti in range((o1 - o0 + 127) // 128)]
            while pend_tiles:
                pend_tiles.pop(0)()
        S.barrier()
        M.release(m0)

    def build(self):
        self.setup()
        done = self.stop_after is not None and self.stop_after[1] == "setup"
        if done:
            self.S.emit()
            return self.nc
        for l in range(self.nlayers):
            for ph in ("a", "mla", "dil", "c"):
                getattr(self, "phase_" + ph)(l)
                if self.stop_after is not None and self.stop_after[0] == l and self.stop_after[1] in (ph, "c1" if ph == "c" else ph):
                    done = True
                    break
            if done:
                break
        self.S.emit()
        return self.nc


def t5_bucket_np(rel):
    nb = 16
    max_exact = 8
    ret = (rel > 0).astype(np.int32) * nb
    n = np.abs(rel)
    v = np.log(np.maximum(n, 1).astype(np.float32) / np.float32(max_exact)) / np.float32(math.log(1024 / max_exact))
    large = max_exact + (v * np.float32(nb - max_exact)).astype(np.int32)
    large = np.minimum(large, nb - 1)
    return ret + np.where(n < max_exact, n, large)


def host_prep(inputs):
    rel_bias = np.asarray(inputs["rel_bias"], np.float32)
    i = np.arange(128)[:, None, None]
    dl = np.arange(3)[None, :, None]
    j = np.arange(128)[None, None, :]
    ridx = (dl - 1) * 128 + i - j
    biasM = np.empty((3, 128, 8, 384), np.float32)
    for di, d in enumerate((1, 4, 16)):
        rel = (ridx * d).astype(np.int32)
        bk = t5_bucket_np(rel)
        vals = rel_bias[bk]
        vals = np.where((np.abs(ridx) <= 64)[..., None], vals, np.float32(-30000.0))
        biasM[di] = np.transpose(vals, (0, 3, 1, 2)).reshape(128, 8, 384)
    invf = np.power(np.float32(10000.0), -np.arange(16, dtype=np.float32) / np.float32(16)).astype(np.float32)
    common = {"biasM": np.ascontiguousarray(biasM.reshape(3, 128, 8 * 384)), "invf": invf}
    for k in ("attn_norm", "w_in", "g_cq", "g_ckv", "w_uq", "w_ukv", "g_mla_q", "g_mla_k", "g_dil_q", "g_dil_k", "w_out",
              "ffn_norm", "w_up", "w_down"):
        common[k] = np.ascontiguousarray(np.asarray(inputs[k], np.float32))
    cwh = np.asarray(inputs["conv_w"], np.float32)
    common["conv_w"] = np.ascontiguousarray(cwh.reshape(2, 3, 44, 128).transpose(0, 3, 1, 2).reshape(2, 128, 132))
    cbh = np.asarray(inputs["conv_b"], np.float32)
    common["conv_b"] = np.ascontiguousarray(cbh.reshape(2, 44, 128).transpose(0, 2, 1))
    x = np.asarray(inputs["x"], np.float32)
    pos = np.asarray(inputs["positions"], np.int32)
    in_maps = []
    for c in range(8):
        m = dict(common)
        m["x"] = np.ascontiguousarray(x[c])
        m["pos"] = np.ascontiguousarray(pos[c].reshape(32, 128).T)
        in_maps.append(m)
    return in_maps


def kernel(**inputs):
    in_maps = host_prep(inputs)
    nc = K().build()
    res = run_bass_kernel_spmd(nc, in_maps, core_ids=list(range(8)))
    return np.stack([np.asarray(r["out"], np.float32) for r in res.results], axis=0)
```
